# Optimizing a Trainium2 kernel written in Bass

```python
import math
import jax
import jax.numpy as jnp
from jax import lax
import numpy as np

D_MODEL = 2048
BATCH = 8
SEQ = 4096
DEPTH = 2
DEC_BATCH = 2
DEC_SEQ = 8192
PAST_LEN = 128

HEAD_DIM = 64
BLOCK = 128
GRID_W = 64
PLE_DIM = 256
EPS = 1e-6
NEG_INF = -1e30

A_HEADS = 8
A_KV_HEADS = 2
A_WINDOW = 128
A_WIDTH = A_HEADS * HEAD_DIM
A_KV_WIDTH = A_KV_HEADS * HEAD_DIM

B_GROUPS = 4
B_GROUP_W = 128
B_WIDTH = B_GROUPS * B_GROUP_W
B_POOL_SIZES = (2, 4, 8, 16)

C_HEADS = 4
C_VDIM = 2 * HEAD_DIM
C_QK_WIDTH = C_HEADS * 2 * HEAD_DIM
C_WIDTH = C_HEADS * C_VDIM

D_HEADS = 8
D_KV_HEADS = 2
D_WIDTH = D_HEADS * HEAD_DIM
D_KV_WIDTH = D_KV_HEADS * HEAD_DIM
ROPE_THETA = 10000.0

REL_BUCKETS = 32
REL_MAX_DIST = 128
REL_HEADS = A_HEADS + C_HEADS

MIX_WIDTH = A_WIDTH + B_WIDTH + C_WIDTH + D_WIDTH
IN_SIZES = (A_WIDTH, A_KV_WIDTH, A_KV_WIDTH, A_WIDTH,
            B_WIDTH, B_WIDTH,
            C_QK_WIDTH, C_QK_WIDTH, C_WIDTH, C_WIDTH,
            D_WIDTH, D_KV_WIDTH, D_KV_WIDTH, D_WIDTH)
IN_WIDTH = 5632

kernel_name = 'hybrid_parallel_group_encoder'


def rms_norm(x, g):
    xf = x.astype(jnp.float32)
    y = xf * lax.rsqrt(jnp.mean(xf * xf, axis=-1, keepdims=True) + EPS)
    return y * g.astype(jnp.float32)


def split_columns(u):
    parts, start = [], 0
    for n in IN_SIZES:
        parts.append(u[..., start:start + n])
        start += n
    return parts


def rel_bucket(rel):
    nb = REL_BUCKETS // 2
    max_exact = nb // 2
    ret = jnp.where(rel > 0, nb, 0)
    n = jnp.abs(rel)
    nf = jnp.maximum(n, 1).astype(jnp.float32)
    large = max_exact + (jnp.log(nf / max_exact) / math.log(REL_MAX_DIST / max_exact)
                         * (nb - max_exact)).astype(jnp.int32)
    large = jnp.minimum(large, nb - 1)
    return ret + jnp.where(n < max_exact, n, large)


def window_attention(q, k, v, sink, rel_tab):
    bsz, s_len = q.shape[:2]
    nb = s_len // BLOCK
    grp = A_HEADS // A_KV_HEADS
    f32 = jnp.float32
    qb = jnp.moveaxis(q.astype(f32).reshape(bsz, nb, BLOCK, A_KV_HEADS, grp, HEAD_DIM), 1, 0)
    pad = ((0, 0), (BLOCK, BLOCK), (0, 0), (0, 0))
    kp = jnp.pad(k.astype(f32), pad)
    vp = jnp.pad(v.astype(f32), pad)
    rel = jnp.arange(3 * BLOCK)[None, :] - BLOCK - jnp.arange(BLOCK)[:, None]
    in_band = jnp.abs(rel) <= A_WINDOW
    bias = jnp.transpose(rel_tab.astype(f32)[rel_bucket(rel)], (2, 0, 1))
    bias = bias.reshape(A_KV_HEADS, grp, BLOCK, 3 * BLOCK)
    sink_col = jnp.broadcast_to(sink.astype(f32).reshape(1, A_KV_HEADS, grp, 1, 1),
                                (bsz, A_KV_HEADS, grp, BLOCK, 1))
    scale = HEAD_DIM ** -0.5

    def one_block(args):
        qi, b = args
        kw = lax.dynamic_slice_in_dim(kp, b * BLOCK, 3 * BLOCK, axis=1)
        vw = lax.dynamic_slice_in_dim(vp, b * BLOCK, 3 * BLOCK, axis=1)
        kpos = b * BLOCK - BLOCK + jnp.arange(3 * BLOCK)
        valid = in_band & ((kpos >= 0) & (kpos < s_len))[None, :]
        sc = jnp.einsum('bqkgd,bjkd->bkgqj', qi, kw) * scale + bias
        sc = jnp.where(valid, sc, NEG_INF)
        pr = jax.nn.softmax(jnp.concatenate([sc, sink_col], axis=-1), axis=-1)[..., :-1]
        return jnp.einsum('bkgqj,bjkd->bqkgd', pr, vw)

    o = lax.map(one_block, (qb, jnp.arange(nb)))
    return jnp.moveaxis(o, 0, 1).reshape(bsz, s_len, A_WIDTH)


def multi_scale_pool(x, w, scale):
    bsz, s_len, _ = x.shape
    xf = x.astype(jnp.float32)
    csum = jnp.concatenate([jnp.zeros((bsz, 1, B_WIDTH), jnp.float32), jnp.cumsum(xf, axis=1)], axis=1)
    t = jnp.arange(s_len)
    outs = []
    for gi, size in enumerate(B_POOL_SIZES):
        lo = size // 2
        hi = size - lo - 1
        start = jnp.maximum(t - lo, 0)
        end = jnp.minimum(t + hi, s_len - 1) + 1
        cg = csum[..., gi * B_GROUP_W:(gi + 1) * B_GROUP_W]
        win_sum = jnp.take(cg, end, axis=1) - jnp.take(cg, start, axis=1)
        outs.append(win_sum / (end - start).astype(jnp.float32)[None, :, None])
    pooled = (jnp.concatenate(outs, axis=-1) - xf).reshape(bsz, s_len, B_GROUPS, B_GROUP_W)
    y = jnp.einsum('bsgc,gce->bsge', pooled, w.astype(jnp.float32)).reshape(bsz, s_len, B_WIDTH)
    return y * scale.astype(jnp.float32)


def diff_attention(q, k, v, lam, rel_tab):
    bsz, s_len = q.shape[:2]
    nb = s_len // BLOCK
    f32 = jnp.float32
    qb = jnp.moveaxis(q.astype(f32).reshape(bsz, nb, BLOCK, C_HEADS, 2, HEAD_DIM), 1, 0)
    kf = k.astype(f32)
    vf = v.astype(f32)
    tab = rel_tab.astype(f32)
    kpos = jnp.arange(s_len)
    scale = HEAD_DIM ** -0.5

    def one_block(args):
        qi, b = args
        qpos = b * BLOCK + jnp.arange(BLOCK)
        bias = jnp.transpose(tab[rel_bucket(kpos[None, :] - qpos[:, None])], (2, 0, 1))[:, None]
        sc = jnp.einsum('bqhmd,bkhmd->bhmqk', qi, kf) * scale + bias
        pr = jax.nn.softmax(sc, axis=-1)
        attn = pr[:, :, 0] - lam * pr[:, :, 1]
        return jnp.einsum('bhqk,bkhe->bqhe', attn, vf)

    o = lax.map(one_block, (qb, jnp.arange(nb)))
    return jnp.moveaxis(o, 0, 1).reshape(bsz, s_len, C_HEADS, C_VDIM)


def axial_rope(s_len):
    rows = s_len // GRID_W
    t_row = jnp.repeat(jnp.arange(rows), GRID_W).astype(jnp.float32)
    t_col = jnp.tile(jnp.arange(GRID_W), rows).astype(jnp.float32)
    half = HEAD_DIM // 2
    inv = ROPE_THETA ** (-jnp.arange(0, half, 2, dtype=jnp.float32) / half)
    ang_r = t_row[:, None] * inv[None, :]
    ang_c = t_col[:, None] * inv[None, :]
    ang = jnp.concatenate([ang_r, ang_r, ang_c, ang_c], axis=-1)[:, None, :]
    return jnp.cos(ang), jnp.sin(ang)


def rotate_half(z):
    h = z.shape[-1] // 2
    return jnp.concatenate([-z[..., h:], z[..., :h]], axis=-1)


def apply_axial_rope(x, cos, sin):
    xr, xc = jnp.split(x, 2, axis=-1)
    xrot = jnp.concatenate([rotate_half(xr), rotate_half(xc)], axis=-1)
    return x * cos + xrot * sin


def grid_attention(q, k, v):
    bsz, s_len = q.shape[:2]
    nb = s_len // BLOCK
    grp = D_HEADS // D_KV_HEADS
    qb = jnp.moveaxis(q.reshape(bsz, nb, BLOCK, D_KV_HEADS, grp, HEAD_DIM), 1, 0)
    scale = HEAD_DIM ** -0.5

    def one_block(qi):
        sc = jnp.einsum('bqkgd,bjkd->bkgqj', qi, k) * scale
        pr = jax.nn.softmax(sc, axis=-1)
        return jnp.einsum('bkgqj,bjkd->bqkgd', pr, v)

    o = lax.map(one_block, qb)
    return jnp.moveaxis(o, 0, 1).reshape(bsz, s_len, D_WIDTH)


def encoder_layer(x, pe, rel_bias, layer_idx, w_in, w_o, g_pre, g_post, sink_a, pool_w, pool_scale,
                  lam_q1, lam_k1, lam_q2, lam_k2, diff_subln, qnorm_d, knorm_d, w_pe, w_pg):
    bsz, s_len, _ = x.shape
    dt = x.dtype
    f32 = jnp.float32
    h = rms_norm(x, g_pre).astype(dt)
    u = h @ w_in
    (aq, ak, av, ag, bx, bg, cq, ck, cv, cg, dq, dk, dv, dg) = split_columns(u)

    def heads(t, n):
        return t.reshape(bsz, s_len, n, -1)

    ya = window_attention(heads(aq, A_HEADS), heads(ak, A_KV_HEADS), heads(av, A_KV_HEADS),
                          sink_a, rel_bias[:, :A_HEADS])
    yb = multi_scale_pool(bx, pool_w, pool_scale)
    lambda_init = 0.8 - 0.6 * math.exp(-0.3 * layer_idx)
    lam = (jnp.exp(jnp.sum(lam_q1.astype(f32) * lam_k1.astype(f32)))
           - jnp.exp(jnp.sum(lam_q2.astype(f32) * lam_k2.astype(f32))) + lambda_init)
    yc = diff_attention(cq.reshape(bsz, s_len, C_HEADS, 2, HEAD_DIM),
                        ck.reshape(bsz, s_len, C_HEADS, 2, HEAD_DIM),
                        heads(cv, C_HEADS), lam, rel_bias[:, A_HEADS:])
    yc = (rms_norm(yc, diff_subln) * (1.0 - lambda_init)).reshape(bsz, s_len, C_WIDTH)
    cos, sin = axial_rope(s_len)
    qd = apply_axial_rope(rms_norm(heads(dq, D_HEADS), qnorm_d), cos, sin)
    kd = apply_axial_rope(rms_norm(heads(dk, D_KV_HEADS), knorm_d), cos, sin)
    yd = grid_attention(qd, kd, heads(dv, D_KV_HEADS).astype(f32))

    mixed = jnp.concatenate([ya.astype(dt) * jax.nn.silu(ag),
                             yb.astype(dt) * jax.nn.silu(bg),
                             yc.astype(dt) * jax.nn.silu(cg),
                             yd.astype(dt) * jax.nn.silu(dg)], axis=-1)
    x = x + rms_norm(mixed @ w_o, g_post).astype(dt)
    gate = jax.nn.sigmoid(x @ w_pg)
    return x + (pe @ w_pe) * gate


def run_trunk(x, p, rel_bias, w_in, w_o, g_pre, g_post, sink_a, pool_w, pool_scale,
              lam_q1, lam_k1, lam_q2, lam_k2, diff_subln, qnorm_d, knorm_d, w_pe, w_pg):
    for i in range(DEPTH):
        x = encoder_layer(x, p[i], rel_bias, i, w_in[i], w_o[i], g_pre[i], g_post[i], sink_a[i],
                          pool_w[i], pool_scale[i], lam_q1[i], lam_k1[i], lam_q2[i], lam_k2[i],
                          diff_subln[i], qnorm_d[i], knorm_d[i], w_pe[i], w_pg[i])
    return x


def setup_inputs(seed: int = 0) -> dict:
    key = jax.random.key(seed)
    ks = jax.random.split(key, 24)
    f32 = jnp.float32

    def nrm(k, shape, s):
        return jax.random.normal(k, shape, f32) * s

    def gain(k, shape):
        return 1.0 + 0.1 * jax.random.normal(k, shape, f32)

    return {
        'x_prompt': nrm(ks[0], (BATCH, SEQ, D_MODEL), 1.0),
        'x_sample': nrm(ks[1], (DEC_BATCH, DEC_SEQ, D_MODEL), 1.0),
        'p_prompt': nrm(ks[2], (DEPTH, BATCH, SEQ, PLE_DIM), 1.0),
        'p_sample': nrm(ks[3], (DEPTH, DEC_BATCH, DEC_SEQ, PLE_DIM), 1.0),
        'w_in': nrm(ks[4], (DEPTH, D_MODEL, IN_WIDTH), D_MODEL ** -0.5),
        'w_o': nrm(ks[5], (DEPTH, MIX_WIDTH, D_MODEL), MIX_WIDTH ** -0.5),
        'g_pre': gain(ks[6], (DEPTH, D_MODEL)),
        'g_post': gain(ks[7], (DEPTH, D_MODEL)),
        'sink_a': nrm(ks[8], (DEPTH, A_HEADS), 0.5),
        'pool_w': nrm(ks[9], (DEPTH, B_GROUPS, B_GROUP_W, B_GROUP_W), B_GROUP_W ** -0.5),
        'pool_scale': gain(ks[10], (DEPTH, B_WIDTH)),
        'lam_q1': nrm(ks[11], (DEPTH, HEAD_DIM), 0.1),
        'lam_k1': nrm(ks[12], (DEPTH, HEAD_DIM), 0.1),
        'lam_q2': nrm(ks[13], (DEPTH, HEAD_DIM), 0.1),
        'lam_k2': nrm(ks[14], (DEPTH, HEAD_DIM), 0.1),
        'diff_subln': gain(ks[15], (DEPTH, C_VDIM)),
        'qnorm_d': gain(ks[16], (DEPTH, HEAD_DIM)),
        'knorm_d': gain(ks[17], (DEPTH, HEAD_DIM)),
        'rel_bias': nrm(ks[18], (REL_BUCKETS, REL_HEADS), 0.5),
        'w_pe': nrm(ks[19], (DEPTH, PLE_DIM, D_MODEL), PLE_DIM ** -0.5),
        'w_pg': nrm(ks[20], (DEPTH, D_MODEL, D_MODEL), D_MODEL ** -0.5),
    }


def reference(x_prompt, x_sample, p_prompt, p_sample, w_in, w_o, g_pre, g_post, sink_a, pool_w,
              pool_scale, lam_q1, lam_k1, lam_q2, lam_k2, diff_subln, qnorm_d, knorm_d, rel_bias,
              w_pe, w_pg):
    y_prompt = run_trunk(x_prompt, p_prompt, rel_bias, w_in, w_o, g_pre, g_post, sink_a, pool_w,
                         pool_scale, lam_q1, lam_k1, lam_q2, lam_k2, diff_subln, qnorm_d, knorm_d,
                         w_pe, w_pg)
    y_sample = run_trunk(x_sample, p_sample, rel_bias, w_in, w_o, g_pre, g_post, sink_a, pool_w,
                         pool_scale, lam_q1, lam_k1, lam_q2, lam_k2, diff_subln, qnorm_d, knorm_d,
                         w_pe, w_pg)
    return (y_prompt, y_sample)
```

```python
import math
from contextlib import ExitStack

import numpy as np
import ml_dtypes

import concourse.bass as bass
import concourse.mybir as mybir
from concourse.bass_utils import run_bass_kernel_spmd

F32 = mybir.dt.float32
BF16 = mybir.dt.bfloat16
ALU = mybir.AluOpType
AF = mybir.ActivationFunctionType
AX = mybir.AxisListType

D = 2048
NKC = 16
INW = 5632
PLE = 256
EPS = 1e-6
BIG = 30000.0
CFG = {"SP": 4096, "SS": 8192, "DEPTH": 2, "NCORES": 8, "STOP": None}

C_AQ, C_AK, C_AG, C_BG, C_CQ, C_CK, C_CG, C_DQ, C_DK, C_DG = 0, 4, 5, 9, 13, 17, 21, 25, 29, 30
NFM = 34


def dev_cols():
    aq, ak, av, ag, bx, bg, cq, ck, cv, cg, dq, dk, dv, dg = (0, 512, 640, 768, 1280, 1792, 2304, 2816, 3328,
                                                               3840, 4352, 4864, 4992, 5120)
    cols = []
    for j in range(4):
        cols += list(range(aq + j * 64, aq + j * 64 + 64)) + list(range(aq + (4 + j) * 64, aq + (5 + j) * 64))
    cols += list(range(ak, ak + 128))
    cols += list(range(ag, ag + 512))
    cols += list(range(bg, bg + 512))
    cols += list(range(cq, cq + 512))
    cols += list(range(ck, ck + 512))
    cols += list(range(cg, cg + 512))
    for j in range(4):
        cols += list(range(dq + j * 64, dq + j * 64 + 64)) + list(range(dq + (4 + j) * 64, dq + (5 + j) * 64))
    cols += list(range(dk, dk + 128))
    cols += list(range(dg, dg + 512))
    assert len(cols) == NFM * 128
    cols += list(range(av, av + 128)) + list(range(dv, dv + 128)) + list(range(cv, cv + 512)) + list(range(bx, bx + 512))
    assert len(cols) == INW and len(set(cols)) == INW
    return np.array(cols)


def rel_bucket_np(rel):
    nb, max_exact = 16, 8
    ret = np.where(rel > 0, nb, 0)
    n = np.abs(rel)
    nf = np.maximum(n, 1).astype(np.float32)
    large = max_exact + (np.log(nf / np.float32(max_exact)) / np.float32(math.log(128 / max_exact))
                         * np.float32(nb - max_exact)).astype(np.int32)
    large = np.minimum(large, nb - 1)
    return ret + np.where(n < max_exact, n, large)


def host_consts(smax):
    c = {}
    rows = smax // 64
    t_row = np.repeat(np.arange(rows), 64).astype(np.float32)
    t_col = np.tile(np.arange(64), rows).astype(np.float32)
    inv = (np.float32(10000.0) ** (-np.arange(0, 32, 2, dtype=np.float32) / np.float32(32))).astype(np.float32)
    ang_r = t_row[:, None] * inv[None, :]
    ang_c = t_col[:, None] * inv[None, :]
    ang = np.concatenate([ang_r, ang_r, ang_c, ang_c], axis=-1).astype(np.float32)
    cos = np.cos(ang).astype(np.float32).T
    sin = np.sin(ang).astype(np.float32).T
    c["cosT"] = np.ascontiguousarray(np.concatenate([cos, cos], 0))
    c["sinT"] = np.ascontiguousarray(np.concatenate([sin, sin], 0))
    rel = np.arange(-255, 256)
    bk = rel_bucket_np(rel)
    erow = np.zeros((33, 511), np.float32)
    erow[bk, np.arange(511)] = 1.0
    erow[32] = (np.abs(rel) > 128).astype(np.float32)
    c["erow"] = erow
    c["maskrow"] = np.array([[-BIG] * 8 + [0.0] * 4], np.float32)
    r64 = np.zeros((64, 64), np.float32)
    for base in (0, 32):
        for i in range(16):
            r64[base + 16 + i, base + i] = -1.0
            r64[base + i, base + 16 + i] = 1.0
    rm = np.zeros((128, 128), np.float32)
    rm[:64, :64] = r64
    rm[64:, 64:] = r64
    c["rmat"] = rm
    c["identf"] = np.eye(128, dtype=np.float32)
    c["identb"] = np.eye(128).astype(ml_dtypes.bfloat16)
    c["onesb"] = np.ones((128, 128)).astype(ml_dtypes.bfloat16)
    bd = np.zeros((128, 128), np.float32)
    bd[:64, :64] = 1.0
    bd[64:, 64:] = 1.0
    c["bdb"] = bd.astype(ml_dtypes.bfloat16)
    s5 = 5 * 128
    mm = np.zeros((128, 20, 128), np.float32)
    t = np.arange(s5)
    for gi, size in enumerate((2, 4, 8, 16)):
        lo = size // 2
        hi = size - lo - 1
        start = np.maximum(t - lo, 0)
        end = np.minimum(t + hi, s5 - 1) + 1
        w = np.zeros((s5, s5), np.float64)
        for ti in range(s5):
            w[ti, start[ti]:end[ti]] = 1.0 / (end[ti] - start[ti])
            w[ti, ti] -= 1.0
        blk = lambda a, b: w[a * 128:(a + 1) * 128, b * 128:(b + 1) * 128].T
        mm[:, gi * 5 + 0, :] = blk(2, 1)
        mm[:, gi * 5 + 1, :] = blk(2, 2)
        mm[:, gi * 5 + 2, :] = blk(2, 3)
        mm[:, gi * 5 + 3, :] = blk(0, 0)
        mm[:, gi * 5 + 4, :] = blk(4, 4)
    c["mmat"] = mm.astype(ml_dtypes.bfloat16)
    return c


class Buf:
    __slots__ = ("w", "r", "name")

    def __init__(self, name=""):
        self.w = None
        self.r = {}
        self.name = name


class Op:
    __slots__ = ("eng", "fn", "deps", "inc", "sem", "semval", "dma", "key")

    def __init__(self, eng, fn, dma):
        self.eng, self.fn, self.dma = eng, fn, dma
        self.inc = dma
        self.sem = None
        self.semval = 0
        self.deps = ()
        self.key = None


class Prog:
    ENGS = ("pe", "act", "dve", "pool", "sp")

    def __init__(self, nc, es):
        self.nc = nc
        self.h = {"pe": nc.tensor, "act": nc.scalar, "dve": nc.vector, "pool": nc.gpsimd, "sp": nc.sync}
        self.csem = {}
        self.cval = {}
        for e in ("pe", "act", "dve", "pool"):
            self.csem[e] = es.enter_context(nc.semaphore("c_" + e))
            self.cval[e] = 0
        self.dsem = {"sp": [], "pool": [], "act": []}
        for q, n in (("sp", 12), ("pool", 8), ("act", 4)):
            for i in range(n):
                self.dsem[q].append(es.enter_context(nc.semaphore("d_%s%d" % (q, i))))
        self.dval = {}
        self.dlast = {}
        self.drr = {"sp": 0, "pool": 0, "act": 0}
        self.seen = {e: {} for e in self.ENGS}
        self.ops = {e: [] for e in self.ENGS}
        self.bufs = []
        self.nops = 0

    def buf(self, name=""):
        b = Buf(name)
        self.bufs.append(b)
        return b

    def op(self, eng, fn, reads=(), writes=(), dma=False):
        o = Op(eng, fn, dma)
        deps = {}
        for b in reads:
            if b.w is not None:
                deps[id(b.w)] = b.w
        for b in writes:
            if b.w is not None:
                deps[id(b.w)] = b.w
            for r in b.r.values():
                deps[id(r)] = r
        if dma:
            lst = self.dsem[eng]
            k = self.drr[eng]
            self.drr[eng] = (k + 1) % len(lst)
            s = lst[k]
            o.sem = s
            o.key = ("d", eng, k)
            prev = self.dlast.get(o.key)
            if prev is not None:
                deps[id(prev)] = prev
            self.dlast[o.key] = o
            self.dval[o.key] = self.dval.get(o.key, 0) + 16
            o.semval = self.dval[o.key]
        else:
            o.sem = self.csem[eng]
            o.key = ("c", eng)
        deps.pop(id(o), None)
        o.deps = tuple(deps.values())
        rk = (eng, id(o)) if dma else eng
        for b in reads:
            b.r[rk] = o
        for b in writes:
            b.w = o
            b.r = {}
        self.ops[eng].append(o)
        self.nops += 1
        return o

    def mm(self, out, lhsT, rhs, start, stop, reads, writes, tp=None):
        if tp is None:
            f = lambda h: h.matmul(out, lhsT=lhsT, rhs=rhs, start=start, stop=stop)
        else:
            f = lambda h: h.matmul(out, lhsT=lhsT, rhs=rhs, start=start, stop=stop, tile_position=tp)
        return self.op("pe", f, reads, writes)

    def tr(self, out, in_, ident, reads, writes):
        return self.op("pe", lambda h: h.transpose(out, in_, ident), reads, writes)

    def act(self, out, in_, func, reads, writes, **kw):
        return self.op("act", lambda h: h.activation(out=out, in_=in_, func=func, **kw), reads, writes)

    def dma(self, q, out, in_, reads, writes):
        return self.op(q, lambda h: h.dma_start(out=out, in_=in_), reads, writes, dma=True)

    def tt(self, eng, out, in0, in1, op, reads, writes):
        return self.op(eng, lambda h: h.tensor_tensor(out=out, in0=in0, in1=in1, op=op), reads, writes)

    def ts(self, eng, out, in0, s1, s2, op0, op1, reads, writes):
        if op1 is None:
            f = lambda h: h.tensor_scalar(out=out, in0=in0, scalar1=s1, scalar2=None, op0=op0)
        else:
            f = lambda h: h.tensor_scalar(out=out, in0=in0, scalar1=s1, scalar2=s2, op0=op0, op1=op1)
        return self.op(eng, f, reads, writes)

    def stt(self, out, in0, scalar, in1, op0, op1, reads, writes, eng="dve"):
        return self.op(eng, lambda h: h.scalar_tensor_tensor(out=out, in0=in0, scalar=scalar, in1=in1,
                                                             op0=op0, op1=op1), reads, writes)

    def cp(self, eng, out, in_, reads, writes):
        if eng == "act":
            return self.act(out, in_, AF.Copy, reads, writes)
        return self.op(eng, lambda h: h.tensor_copy(out=out, in_=in_), reads, writes)

    def recip(self, out, in_, reads, writes):
        return self.op("dve", lambda h: h.reciprocal(out=out, in_=in_), reads, writes)

    def memset(self, eng, ap, val, writes):
        return self.op(eng, lambda h: h.memset(ap, val), (), writes)

    def flush(self):
        nc = self.nc
        for e in self.ENGS:
            for o in self.ops[e]:
                for d in o.deps:
                    if not d.dma:
                        d.inc = True
        for e in ("pe", "act", "dve", "pool"):
            v = self.cval[e]
            for o in self.ops[e]:
                if not o.dma and o.inc:
                    v += 1
                    o.semval = v
            self.cval[e] = v
        prog = self

        def emit(e, h):
            seen = prog.seen[e]
            for o in prog.ops[e]:
                for d in o.deps:
                    if (not d.dma) and d.eng == e and e == "pe":
                        continue
                    if seen.get(d.key, 0) >= d.semval:
                        continue
                    h.wait_ge(d.sem, d.semval)
                    seen[d.key] = d.semval
                ins = o.fn(h)
                if o.inc:
                    ins.then_inc(o.sem, 16 if o.dma else 1)
            for key, last in prog.dlast.items():
                if key[1] == e and seen.get(key, 0) < prog.dval[key]:
                    h.wait_ge(last.sem, prog.dval[key])
                    seen[key] = prog.dval[key]

        with nc.Block() as block:
            if self.ops["sp"]:
                block.sync(lambda h: emit("sp", h))
            if self.ops["pe"]:
                block.tensor(lambda h: emit("pe", h))
            if self.ops["act"]:
                block.scalar(lambda h: emit("act", h))
            if self.ops["dve"]:
                block.vector(lambda h: emit("dve", h))
            if self.ops["pool"]:
                block.gpsimd(lambda h: emit("pool", h))
        self.ops = {e: [] for e in self.ENGS}
        for b in self.bufs:
            b.w = None
            b.r = {}
        self.bufs = []


class TileAlloc:
    def __init__(self, nc, prog):
        self.nc, self.prog = nc, prog
        self.es = ExitStack()

    def __enter__(self):
        self.es.__enter__()
        return self

    def __exit__(self, *a):
        return self.es.__exit__(*a)

    _uid = [0]

    def _nm(self, name):
        TileAlloc._uid[0] += 1
        return "t%d_%s" % (TileAlloc._uid[0], name)

    def sb(self, name, shape, dt):
        return self.es.enter_context(self.nc.sbuf_tensor(self._nm(name), list(shape), dt))

    def ps(self, name, shape, dt):
        return self.es.enter_context(self.nc.psum_tensor(self._nm(name), list(shape), dt))


def build_program(SP, SS, DEPTH):
    nc = bass.Bass("TRN2", target_bir_lowering=False)
    SMAX = max(SP, SS)
    seqs = [("p", SP), ("s", SS)]

    def din(name, shape, dt=F32):
        return nc.dram_tensor(name, list(shape), dt, kind="ExternalInput").ap()

    def dscr(name, shape, dt):
        return nc.dram_tensor(name, list(shape), dt).ap()

    x_in = {"p": din("x_p", [SP, D]), "s": din("x_s", [SS, D])}
    peT_in = {"p": din("peT_p", [DEPTH, PLE, SP]), "s": din("peT_s", [DEPTH, PLE, SS])}
    w_in = din("w_in", [DEPTH, D, INW])
    w_o = din("w_o", [DEPTH, D, D])
    w_pg = din("w_pg", [DEPTH, D, D])
    w_pe = din("w_pe", [DEPTH, PLE, D])
    gpreT = din("gpreT", [DEPTH, 128, 16])
    g_post = din("g_post", [DEPTH, D])
    sink = din("sink", [DEPTH, 8])
    pool_w = din("pool_w", [DEPTH, 4, 128, 128])
    pscaleT = din("pscaleT", [DEPTH, 128, 4])
    lam4 = din("lam4", [DEPTH, 4, 64])
    subln = din("subln", [DEPTH, 128, 1])
    qn128 = din("qn128", [DEPTH, 128, 1])
    kn128 = din("kn128", [DEPTH, 128, 1])
    relb = din("relb", [32, 12])
    cosT = din("cosT", [128, SMAX])
    sinT = din("sinT", [128, SMAX])
    erow_d = din("erow", [33, 511])
    maskrow = din("maskrow", [1, 12])
    rmat_d = din("rmat", [128, 128])
    identf_d = din("identf", [128, 128])
    identb_d = din("identb", [128, 128], BF16)
    onesb_d = din("onesb", [128, 128], BF16)
    bdb_d = din("bdb", [128, 128], BF16)
    mmat_d = din("mmat", [128, 20, 128], BF16)
    rowidx_d = din("rowidx", [128, SS // 128], mybir.dt.int32)
    isr_d = din("isr", [1, SS // 128])
    flags_d = din("flags", [1, 4])
    mmatv_d = din("mmatv", [128, 16, 128], BF16)
    cosT2 = din("cosT2", [128, SS])
    sinT2 = din("sinT2", [128, SS])
    peT_s2 = din("peT_s2", [PLE, SS // 4])
    y_out = {"p": nc.dram_tensor("y_p", [SP, D], F32, kind="ExternalOutput").ap(),
             "s": nc.dram_tensor("y_s", [SS // 4, D], F32, kind="ExternalOutput").ap()}

    wc = dscr("wc", [DEPTH, NFM, 128, 2048], BF16)
    wtm = dscr("wtm", [DEPTH, 5, 128, 16, 256], BF16)
    wo_s = dscr("wo_s", [DEPTH, 16, 128, 2048], BF16)
    wpg_s = dscr("wpg_s", [DEPTH, 16, 128, 2048], BF16)
    uT, VA, VD, VC, BX, x1s = {}, {}, {}, {}, {}, {}
    for k, S in seqs:
        G = S // 1024
        uT[k] = dscr("uT_" + k, [NFM, 128, S], BF16)
        VA[k] = dscr("VA_" + k, [G, 128, 8, 384], BF16)
        VD[k] = dscr("VD_" + k, [G, 128, 8, 384], BF16)
        VC[k] = dscr("VC_" + k, [4, G, 128, 8, 192], BF16)
        BX[k] = dscr("BX_" + k, [S, 512], BF16)
        x1s[k] = [dscr("x1a_" + k, [S, D], F32), dscr("x1b_" + k, [S, D], F32)]

    with ExitStack() as es:
        P = Prog(nc, es)
        pers = TileAlloc(nc, P)
        es.enter_context(pers)
        identb = pers.sb("identb", [128, 128], BF16)
        onesb = pers.sb("onesb", [128, 128], BF16)
        bdb = pers.sb("bdb", [128, 128], BF16)
        rmat = pers.sb("rmat", [128, 128], F32)
        mmat = pers.sb("mmat", [128, 20, 128], BF16)
        strips = pers.sb("strips", [128, 8, 3, 128], F32)
        stripsC = pers.sb("stripsC", [128, 4, 9 * 128], F32)
        esink_bc = pers.sb("esink_bc", [128, 8, 128], F32)
        farb = pers.sb("farb", [128, 2, 12], F32)
        wpe_b = pers.sb("wpe_b", [128, 2, 2048], BF16)
        gpost_bc = pers.sb("gpost_bc", [128, 2048], F32)
        poolw_b = pers.sb("poolw_b", [128, 4, 128], BF16)
        pscale_t = pers.sb("pscale_t", [128, 4], F32)
        gq_t = pers.sb("gq_t", [128, 1], F32)
        gk_t = pers.sb("gk_t", [128, 1], F32)
        esink = pers.sb("esink", [128, 8], F32)
        neglam = pers.sb("neglam", [128, 1], F32)
        subg = pers.sb("subg", [128, 1], F32)

        with TileAlloc(nc, P) as ta:
            erow = ta.sb("erow", [33, 511], F32)
            strips_all = ta.sb("strips_all", [128, 12, 3, 128], F32)
            tabext = ta.sb("tabext", [33, 12], F32)
            psb = [ta.ps("ips%d" % i, [128, 512], F32) for i in range(2)]
            b_c = P.buf()
            for dst, src in ((identb, identb_d), (onesb, onesb_d), (bdb, bdb_d),
                             (rmat, rmat_d), (erow, erow_d)):
                P.dma("sp", dst[:], src[:, :], [], [b_c])
            P.dma("sp", mmat[:], mmat_d[:, :, :], [], [b_c])
            P.dma("sp", tabext[0:32, :], relb[:, :], [], [b_c])
            P.dma("sp", tabext[32:33, :], maskrow[:, :], [], [b_c])
            P.dma("sp", farb[:, 0, :], relb[15:16, :].partition_broadcast(128), [], [b_c])
            P.dma("sp", farb[:, 1, :], relb[31:32, :].partition_broadcast(128), [], [b_c])
            b_str = P.buf()
            bps = [P.buf(), P.buf()]
            nb = 0
            for di in range(3):
                dl = di - 1
                for q0 in range(0, 128, 32):
                    ps = psb[nb % 2]
                    bp = bps[nb % 2]
                    nb += 1
                    for ci in range(32):
                        qq = q0 + ci
                        w0 = 128 * dl - qq + 255
                        P.mm(ps[:, ci * 12:(ci + 1) * 12], erow[0:33, w0:w0 + 128], tabext[0:33, :], True, True,
                             [b_c], [bp])
                    P.cp("dve", strips_all[:, :, di, q0:q0 + 32],
                         ps[:, 0:384].rearrange("p (c h) -> p h c", h=12), [bp], [b_str])
            b_sc9 = P.buf()
            P.cp("dve", strips[:], strips_all[:, 0:8, :, :], [b_str], [b_sc9])
            for h in range(4):
                fin0 = mmat[:, 0:3, :].rearrange("p a b -> p (a b)")
                P.ts("dve", stripsC[:, h, 0:384], fin0, 0.0, farb[:, 1, 8 + h:9 + h], ALU.mult, ALU.add,
                     [b_str, b_c], [b_sc9])
                P.ts("dve", stripsC[:, h, 768:1152], fin0, 0.0, farb[:, 0, 8 + h:9 + h], ALU.mult, ALU.add,
                     [b_str, b_c], [b_sc9])
                for dl in (1, 0, -1):
                    blk = 4 - dl
                    P.cp("dve", stripsC[:, h, blk * 128:(blk + 1) * 128], strips_all[:, 8 + h, dl + 1, :], [b_str], [b_sc9])
            P.flush()
        stage = [0]

        def stop_now():
            stage[0] += 1
            return CFG.get("STOP") is not None and stage[0] > CFG["STOP"]

        for l in range(DEPTH):
            if stop_now():
                break
            lam_init = 0.8 - 0.6 * math.exp(-0.3 * l)
            with TileAlloc(nc, P) as ta:
                gpre_t = ta.sb("gpre_t", [128, 16], F32)
                wld = [ta.sb("wld%d" % i, [128, 16, 128], F32) for i in range(2)]
                wcv = [ta.sb("wcv%d" % i, [128, 16, 128], BF16) for i in range(2)]
                rld = [ta.sb("rld%d" % i, [128, 2048], F32) for i in range(2)]
                rcv = [ta.sb("rcv%d" % i, [128, 2048], BF16) for i in range(2)]
                pwl = ta.sb("pwl", [128, 4, 128], F32)
                pel = ta.sb("pel", [128, 2, 2048], F32)
                lamt = ta.sb("lamt", [128, 4, 64], F32)
                lamp = ta.sb("lamp", [128, 2, 64], F32)
                lams = ta.sb("lams", [128, 2], F32)
                lame = ta.sb("lame", [128, 2], F32)
                sinkt = ta.sb("sinkt", [128, 8], F32)
                qnt = ta.sb("qnt", [128, 1], F32)
                sublt = ta.sb("sublt", [128, 1], F32)
                b_g = P.buf()
                P.dma("sp", gpre_t[:], gpreT[l, :, :], [], [b_g])
                b_wld = [P.buf(), P.buf()]
                b_wcv = [P.buf(), P.buf()]
                cnt = 0
                for c in range(44):
                    s = c % 2
                    P.dma("sp", wld[s][:], w_in[l, :, c * 128:(c + 1) * 128].rearrange("(kc p) m -> p kc m", p=128),
                          [], [b_wld[s]])
                    for kc in range(NKC):
                        if cnt % 2 == 0:
                            P.ts("dve", wcv[s][:, kc, :], wld[s][:, kc, :], gpre_t[:, kc:kc + 1], None, ALU.mult, None,
                                 [b_wld[s], b_g], [b_wcv[s]])
                        else:
                            P.act(wcv[s][:, kc, :], wld[s][:, kc, :], AF.Copy, [b_wld[s], b_g], [b_wcv[s]],
                                  scale=gpre_t[:, kc:kc + 1])
                        cnt += 1
                    if c < NFM:
                        P.dma("pool", wc[l, c, :, :], wcv[s][:].rearrange("p k m -> p (k m)"), [b_wcv[s]], [])
                    else:
                        g, hf = (c - NFM) // 2, (c - NFM) % 2
                        P.dma("pool", wtm[l, g, :, :, hf * 128:(hf + 1) * 128], wcv[s][:], [b_wcv[s]], [])
                b_rld = [P.buf(), P.buf()]
                b_rcv = [P.buf(), P.buf()]
                cnt = 0
                for src, dst in ((w_o, wo_s), (w_pg, wpg_s)):
                    for kc in range(NKC):
                        s = cnt % 2
                        P.dma("sp", rld[s][:], src[l, kc * 128:(kc + 1) * 128, :], [], [b_rld[s]])
                        P.cp("dve" if cnt % 2 == 0 else "act", rcv[s][:], rld[s][:], [b_rld[s]], [b_rcv[s]])
                        P.dma("pool", dst[l, kc, :, :], rcv[s][:], [b_rcv[s]], [])
                        cnt += 1
                b_v = P.buf()
                b_pl = P.buf()
                P.dma("sp", pel[:], w_pe[l, :, :].rearrange("(k p) n -> p k n", p=128), [], [b_pl])
                P.cp("dve", wpe_b[:], pel[:], [b_pl], [b_v])
                P.dma("sp", pwl[:], pool_w[l, :, :, :].rearrange("g c e -> c g e"), [], [b_pl])
                P.cp("dve", poolw_b[:], pwl[:], [b_pl], [b_v])
                P.dma("sp", gpost_bc[:], g_post[l:l + 1, :].partition_broadcast(128), [], [b_v])
                P.dma("sp", pscale_t[:], pscaleT[l, :, :], [], [b_v])
                P.dma("sp", gk_t[:], kn128[l, :, :], [], [b_v])
                P.dma("sp", qnt[:], qn128[l, :, :], [], [b_pl])
                P.ts("dve", gq_t[:], qnt[:], 0.125, None, ALU.mult, None, [b_pl], [b_v])
                P.dma("sp", sublt[:], subln[l, :, :], [], [b_pl])
                P.ts("dve", subg[:], sublt[:], float(1.0 - lam_init), None, ALU.mult, None, [b_pl], [b_v])
                P.dma("sp", sinkt[:], sink[l:l + 1, :].partition_broadcast(128), [], [b_pl])
                P.act(esink[:], sinkt[:], AF.Exp, [b_pl], [b_v])
                for h in range(8):
                    P.ts("dve", esink_bc[:, h, :], mmat[:, 0, :], 0.0, esink[:, h:h + 1], ALU.mult, ALU.add,
                         [b_v], [b_v])
                for i in range(4):
                    P.dma("sp", lamt[:, i, :], lam4[l, i:i + 1, :].partition_broadcast(128), [], [b_pl])
                b_l = P.buf()
                for i in range(2):
                    P.tt("dve", lamp[:, i, :], lamt[:, 2 * i, :], lamt[:, 2 * i + 1, :], ALU.mult, [b_pl], [b_l])
                    P.op("dve", lambda h, i=i: h.reduce_sum(out=lams[:, i:i + 1], in_=lamp[:, i, :], axis=AX.X),
                         [b_l], [b_l])
                P.act(lame[:], lams[:], AF.Exp, [b_l], [b_l])
                P.tt("dve", neglam[:], lame[:, 1:2], lame[:, 0:1], ALU.subtract, [b_l], [b_v])
                P.ts("dve", neglam[:], neglam[:], float(-lam_init), None, ALU.add, None, [b_v], [b_v])
                P.flush()

            for (sk, S) in seqs:
                xsrc = x_in[sk] if l == 0 else x1s[sk][(l - 1) % 2]
                xdst = y_out[sk] if l == DEPTH - 1 else x1s[sk][l % 2]
                if stop_now():
                    break
                view = (sk == "s" and l == DEPTH - 1)
                phase1(nc, P, l, sk, S, xsrc, locals(), view)
                if stop_now():
                    break
                phase2(nc, P, l, sk, S, xsrc, xdst, locals(), view)
    return nc


def phase1(nc, P, l, sk, S, xsrc, E, view=False):
    identb, bdb, rmat = E["identb"], E["bdb"], E["rmat"]
    gq_t, gk_t = E["gq_t"], E["gk_t"]
    wc, wtm = E["wc"], E["wtm"]
    uT, VA, VD, VC, BX = E["uT"][sk], E["VA"][sk], E["VD"][sk], E["VC"][sk], E["BX"][sk]
    cosT, sinT = (E["cosT2"], E["sinT2"]) if view else (E["cosT"], E["sinT"])
    NST = S // 1024
    KSET = (C_AK, C_CK, C_CK + 1, C_CK + 2, C_CK + 3, C_DK)
    with TileAlloc(nc, P) as ta:
        xin = [ta.sb("xin%d" % i, [128, 2048], F32) for i in range(2)]
        xsb = ta.sb("xsb", [128, 2048], BF16)
        ssq = [ta.sb("ssq%d" % i, [128, 1], F32) for i in range(2)]
        sdv = [ta.sb("sdv%d" % i, [128, 1], F32) for i in range(2)]
        rsv = [ta.sb("rsv%d" % i, [128, 1], F32) for i in range(2)]
        hT = ta.sb("hT", [128, 16, 1024], BF16)
        wsl = [ta.sb("wsl%d" % i, [128, 16, 128], BF16) for i in range(3)]
        wts = [ta.sb("wts%d" % i, [128, 16, 256], BF16) for i in range(2)]
        stg = [ta.sb("stg%d" % i, [128, 1024], BF16) for i in range(3)]
        vsAD = ta.sb("vsAD", [128, 4, 4, 3, 64], BF16)
        vsC = ta.sb("vsC", [128, 4, 4, 3, 64], BF16)
        bst = ta.sb("bst", [128, 4, 512], BF16)
        sq = ta.sb("sq", [128, 512], BF16)
        sdt = ta.sb("sdt", [128, 512], F32)
        rst = ta.sb("rst", [128, 512], F32)
        xn = ta.sb("xn", [128, 512], F32)
        t1 = ta.sb("t1", [128, 512], F32)
        cs = ta.sb("cs", [128, 1024], F32)
        sn = ta.sb("sn", [128, 1024], F32)
        ptp = [ta.ps("ptp%d" % i, [128, 1024], BF16) for i in range(2)]
        pf = [ta.ps("pf%d" % i, [128, 512], F32) for i in range(4)]
        pm = [ta.ps("pm%d" % i, [128, 512], F32) for i in range(2)]

        b_xin = [P.buf(), P.buf()]
        b_xsb = P.buf()
        b_st = [P.buf(), P.buf()]
        b_hT = [P.buf() for _ in range(8)]
        b_idx = P.buf()
        if view:
            idxt = ta.sb("idxt", [128, S // 128], mybir.dt.int32)
            P.dma("sp", idxt[:], E["rowidx_d"][:, :], [], [b_idx])
        b_wsl = [P.buf() for _ in range(3)]
        b_wts = [P.buf() for _ in range(2)]
        b_stg = [P.buf() for _ in range(3)]
        b_vs = P.buf()
        b_bst = P.buf()
        b_ptp = [P.buf(), P.buf()]
        b_pf = [P.buf() for _ in range(4)]
        b_pm = [P.buf(), P.buf()]
        b_d = P.buf()
        b_cs = P.buf()
        b_const = P.buf()

        P.memset("dve", vsAD[:, :, :, 1, :], 1.0, [b_vs])
        P.memset("dve", vsC[:, :, :, 2, :], 1.0, [b_vs])

        ev = 0
        for st in range(NST):
            t0 = st * 1024
            for blk in range(8):
                s = blk % 2
                r0 = t0 + blk * 128
                if view:
                    bi_ = r0 // 128
                    P.op("pool", lambda h, s=s, bi_=bi_: h.indirect_dma_start(
                        out=xin[s][:], out_offset=None, in_=xsrc[:, :],
                        in_offset=bass.IndirectOffsetOnAxis(ap=idxt[:, bi_:bi_ + 1], axis=0)),
                        [b_idx], [b_xin[s]], dma=True)
                else:
                    P.dma("sp", xin[s][:], xsrc[r0:r0 + 128, :], [], [b_xin[s]])
                P.act(xsb[:], xin[s][:], AF.Square, [b_xin[s]], [b_xsb, b_st[s]], accum_out=ssq[s][:])
                P.act(sdv[s][:], ssq[s][:], AF.Sqrt, [b_st[s]], [b_st[s]], scale=1.0 / D, bias=EPS)
                P.recip(rsv[s][:], sdv[s][:], [b_st[s]], [b_st[s]])
                P.ts("dve", xsb[:], xin[s][:], rsv[s][:, 0:1], None, ALU.mult, None, [b_xin[s], b_st[s]], [b_xsb])
                for g in range(2):
                    for i in range(8):
                        kc = g * 8 + i
                        P.tr(ptp[g][:, i * 128:(i + 1) * 128], xsb[:, kc * 128:(kc + 1) * 128], identb[:],
                             [b_xsb, b_const], [b_ptp[g]])
                    P.cp("dve" if g == 0 else "act", hT[:, g * 8:(g + 1) * 8, blk * 128:(blk + 1) * 128],
                         ptp[g][:].rearrange("p (k t) -> p k t", k=8), [b_ptp[g]], [b_hT[blk]])
            P.dma("sp", cs[:], cosT[:, t0:t0 + 1024], [], [b_cs])
            P.dma("sp", sn[:], sinT[:, t0:t0 + 1024], [], [b_cs])
            skip_nonk = view and (S // 4) % 1024 == 0 and st >= (S // 4096)
            pend1, pend2 = [], []

            def run_pending():
                for f in pend2[:]:
                    pend2.remove(f)
                    f()
                for f in pend1[:]:
                    pend1.remove(f)
                    f()

            for c in range(NFM):
                if skip_nonk and c not in KSET:
                    continue
                ws = c % 3
                sg = c % 3
                P.dma("sp", wsl[ws][:], wc[l, c, :, :].rearrange("p (k m) -> p k m", k=16), [], [b_wsl[ws]])
                for tt in range(2):
                    pi = (2 * c + tt) % 4
                    for kc in range(NKC):
                        P.mm(pf[pi][:], wsl[ws][:, kc, :], hT[:, kc, tt * 512:(tt + 1) * 512], kc == 0, kc == NKC - 1,
                             [b_wsl[ws]] + b_hT[tt * 4:tt * 4 + 4], [b_pf[pi]])
                    run_pending()

                    def evac(c=c, tt=tt, pi=pi, sg=sg):
                        nonlocal ev
                        dst = stg[sg][:, tt * 512:(tt + 1) * 512]

                        def store():
                            if tt == 1:
                                P.dma("pool", uT[c, :, t0:t0 + 1024], stg[sg][:], [b_stg[sg]], [])
                        if c < C_AK or C_CQ <= c < C_CK:
                            if ev % 2 == 0:
                                P.ts("dve", dst, pf[pi][:], 0.125, None, ALU.mult, None, [b_pf[pi]], [b_stg[sg]])
                            else:
                                P.act(dst, pf[pi][:], AF.Copy, [b_pf[pi]], [b_stg[sg]], scale=0.125)
                            ev += 1
                            store()
                        elif c == C_AK or C_CK <= c < C_CG:
                            P.cp("dve" if ev % 2 == 0 else "act", dst, pf[pi][:], [b_pf[pi]], [b_stg[sg]])
                            ev += 1
                            store()
                        elif c < C_DQ or c >= C_DG:
                            P.act(dst, pf[pi][:], AF.Silu, [b_pf[pi]], [b_stg[sg]])
                            store()
                        else:
                            gt = gq_t if c < C_DK else gk_t
                            P.act(sq[:], pf[pi][:], AF.Square, [b_pf[pi]], [b_d])
                            P.mm(pm[0][:], bdb[:], sq[:], True, True, [b_d, b_const], [b_pm[0]])
                            P.act(sdt[:], pm[0][:], AF.Sqrt, [b_pm[0]], [b_d], scale=1.0 / 64, bias=EPS)
                            P.recip(rst[:], sdt[:], [b_d], [b_d])
                            P.stt(xn[:], pf[pi][:], gt[:, 0:1], rst[:], ALU.mult, ALU.mult, [b_pf[pi], b_d], [b_d])

                            def stage2():
                                P.mm(pm[1][:], rmat[:], xn[:], True, True, [b_d, b_const], [b_pm[1]])
                                P.tt("dve", t1[:], xn[:], cs[:, tt * 512:(tt + 1) * 512], ALU.mult, [b_d, b_cs], [b_d])
                                P.tt("dve", sdt[:], pm[1][:], sn[:, tt * 512:(tt + 1) * 512], ALU.mult,
                                     [b_pm[1], b_cs, b_d], [b_d])
                                P.tt("dve", dst, t1[:], sdt[:], ALU.add, [b_d], [b_stg[sg]])
                                store()
                            pend2.append(stage2)
                    pend1.append(evac)
            run_pending()
            run_pending()
            for g in range(5):
                wsi = g % 2
                P.dma("sp", wts[wsi][:], wtm[l, g, :, :, :], [], [b_wts[wsi]])
                for blk in range(8):
                    pi = blk % 2
                    bq = blk % 4
                    for kc in range(NKC):
                        P.mm(pm[pi][:, 0:256], hT[:, kc, blk * 128:(blk + 1) * 128], wts[wsi][:, kc, :], kc == 0,
                             kc == NKC - 1, [b_wts[wsi], b_hT[blk]], [b_pm[pi]])
                    eng = "dve" if blk % 2 == 0 else "act"
                    if g == 0:
                        src = pm[pi][:, 0:256].rearrange("p (h d) -> p h d", h=4)
                        P.cp(eng, vsAD[:, bq, :, 0, :], src, [b_pm[pi]], [b_vs])
                        P.cp("dve" if eng == "act" else "act", vsAD[:, bq, :, 2, :], src, [b_pm[pi]], [b_vs])
                    elif g in (1, 2):
                        h0 = (g - 1) * 2
                        src = pm[pi][:, 0:256].rearrange("p (h a d) -> p h a d", h=2, a=2)
                        P.cp(eng, vsC[:, h0:h0 + 2, bq, 0:2, :], src, [b_pm[pi]], [b_vs])
                    else:
                        c0 = (g - 3) * 256
                        P.cp(eng, bst[:, bq, c0:c0 + 256], pm[pi][:, 0:256], [b_pm[pi]], [b_bst])
                    if bq == 3:
                        k0 = (blk // 4) * 4
                        if g == 0:
                            P.dma("pool", VA[st, :, k0:k0 + 4, :],
                                  vsAD[:, :, 0:2, :, :].rearrange("p b h a d -> p b (h a d)"), [b_vs], [])
                            P.dma("pool", VD[st, :, k0:k0 + 4, :],
                                  vsAD[:, :, 2:4, :, :].rearrange("p b h a d -> p b (h a d)"), [b_vs], [])
                        elif g in (1, 2):
                            h0 = (g - 1) * 2
                            for hh in range(2):
                                P.dma("pool", VC[h0 + hh, st, :, k0:k0 + 4, :],
                                      vsC[:, h0 + hh, :, :, :].rearrange("p b a d -> p b (a d)"), [b_vs], [])
                        else:
                            rb = t0 + k0 * 128
                            c0 = (g - 3) * 256
                            P.dma("pool", BX[rb:rb + 512, c0:c0 + 256].rearrange("(b p) c -> p b c", p=128),
                                  bst[:, :, c0:c0 + 256], [b_bst], [])
        P.flush()


def phase2(nc, P, l, sk, S, xsrc, xdst, E, view=False):
    identb, onesb, mmat = E["identb"], E["onesb"], E["mmat"]
    strips, farb, wpe_b, gpost_bc, poolw_b = E["strips"], E["farb"], E["wpe_b"], E["gpost_bc"], E["poolw_b"]
    stripsC, esink_bc = E["stripsC"], E["esink_bc"]
    pscale_t, esink, neglam, subg = E["pscale_t"], E["esink"], E["neglam"], E["subg"]
    wo_s, wpg_s = E["wo_s"], E["wpg_s"]
    uT, VA, VD, VC, BX = E["uT"][sk], E["VA"][sk], E["VD"][sk], E["VC"][sk], E["BX"][sk]
    peT = E["peT_in"][sk]
    NQT = S // 512
    NKB = S // 128
    NG = S // 1024
    NQR = NQT // 4 if view else NQT
    with TileAlloc(nc, P) as ta:
        qt_t = [ta.sb("qt%d" % i, [128, 4, 512], BF16) for i in range(2)]
        gt_t = [ta.sb("gt%d" % i, [128, 4, 512], BF16) for i in range(2)]
        kt_t = [ta.sb("kt%d" % i, [128, 1024], BF16) for i in range(3)]
        v_t = [ta.sb("v%d" % i, [128, 8, 384], BF16) for i in range(3)]
        pt_t = [ta.sb("pt%d" % i, [128, 1024], BF16) for i in range(3)]
        bigT = ta.sb("bigT", [128, 16, 512], BF16)
        bxs = ta.sb("bxs", [128, 6, 512], BF16)
        pooled = [ta.sb("pooled%d" % i, [128, 512], BF16) for i in range(1)]
        ysq = ta.sb("ysq", [128, 512], BF16)
        NWPS, NWOS = (3, 2) if view else (4, 3)
        wps = [ta.sb("wps%d" % i, [128, 512], BF16) for i in range(NWPS)]
        wos = [ta.sb("wos%d" % i, [128, 1024], BF16) for i in range(NWOS)]
        peb = ta.sb("peb", [128, 2, 512], BF16)
        pef = ta.sb("pef", [128, 2, 512], F32)
        osb = [ta.sb("osb%d" % i, [128, 2048], F32) for i in range(4)]
        xres = ta.sb("xres", [128, 2048], F32)
        x1b = ta.sb("x1b", [128, 2048], BF16)
        sc = [ta.sb("sc%d" % i, [128, 512], F32) for i in range(3)]
        tmp = [ta.sb("tmp%d" % i, [128, 1024], F32) for i in range(2)]
        tmpf = ta.sb("tmpf", [128, 512], F32)
        ssq = ta.sb("ssq", [128, 16], F32)
        ssr = ta.sb("ssr", [128, 4], F32)
        sdr = ta.sb("sdr", [128, 4], F32)
        rsr = ta.sb("rsr", [128, 4], F32)
        st_ps = [ta.ps("st%d" % i, [128, 1024], F32) for i in range(2)]
        ac_ps = [ta.ps("ac%d" % i, [128, 1024], F32) for i in range(2)]

        bank = [st_ps[0][:, 0:512], st_ps[0][:, 512:1024], st_ps[1][:, 0:512], st_ps[1][:, 512:1024],
                ac_ps[0][:, 0:512], ac_ps[0][:, 512:1024], ac_ps[1][:, 0:512], ac_ps[1][:, 512:1024]]
        b_bank = [P.buf() for _ in range(8)]
        b_qt = [P.buf(), P.buf()]
        b_gt = [P.buf(), P.buf()]
        b_kt = [P.buf() for _ in range(3)]
        b_v = [P.buf() for _ in range(3)]
        b_pt = [P.buf() for _ in range(3)]
        b_big = [P.buf() for _ in range(16)]
        b_bxs = P.buf()
        b_pooled = [P.buf()]
        b_ysq = P.buf()
        b_junk = P.buf()
        b_wps = [P.buf() for _ in range(NWPS)]
        b_wos = [P.buf() for _ in range(NWOS)]
        b_pe = P.buf()
        b_osb = [P.buf() for _ in range(4)]
        b_xres = P.buf()
        b_x1b = P.buf()
        b_sc = [P.buf() for _ in range(3)]
        b_tmp = [P.buf(), P.buf()]
        b_tmp2 = [[P.buf(), P.buf()], [P.buf(), P.buf()]]
        b_tmpf = P.buf()
        b_ss = P.buf()
        b_const = P.buf()
        rr = {"qt": 0, "gt": 0, "kt": 0, "v": 0, "pt": 0, "st": 0, "sc": 0, "tmp": 0, "wps": 0, "wos": 0, "acc": 0}
        b_view = P.buf()
        if view:
            idxt = ta.sb("idxt2", [128, S // 128], mybir.dt.int32)
            farbC = ta.sb("farbC", [128, NKB, 4], F32)
            isrt = ta.sb("isrt", [128, NKB], F32)
            flg = ta.sb("flg", [128, 4], F32)
            omf = ta.sb("omf", [128, 4], F32)
            drl = ta.sb("drl", [128, 4], F32)
            cseam = ta.sb("cseam", [128, 8], F32)
            mmv = ta.sb("mmv", [128, 16, 128], BF16)
            b_vl = P.buf()
            P.dma("sp", idxt[:], E["rowidx_d"][:, :], [], [b_view])
            P.dma("sp", isrt[:], E["isr_d"][0:1, :].partition_broadcast(128), [], [b_vl])
            P.dma("sp", flg[:], E["flags_d"][0:1, :].partition_broadcast(128), [], [b_vl])
            P.dma("sp", mmv[:], E["mmatv_d"][:, :, :], [], [b_view])
            P.tt("dve", drl[:], farb[:, 1, 8:12], farb[:, 0, 8:12], ALU.subtract, [b_const], [b_vl])
            for h in range(4):
                P.ts("dve", farbC[:, :, h], isrt[:], drl[:, h:h + 1], farb[:, 0, 8 + h:9 + h], ALU.mult, ALU.add,
                     [b_vl, b_const], [b_view])
            P.ts("dve", omf[:], flg[:], -1.0, 1.0, ALU.mult, ALU.add, [b_vl], [b_vl])
            P.ts("dve", cseam[:, 0:4], farb[:, 1, 8:12], omf[:, 0:1], None, ALU.mult, None, [b_vl, b_const], [b_view])
            P.ts("dve", cseam[:, 4:8], farb[:, 0, 8:12], omf[:, 1:2], None, ALU.mult, None, [b_vl, b_const], [b_view])

        def nxt(k, n):
            v = rr[k]
            rr[k] = (v + 1) % n
            return v

        def attention(kind, qsl, qi, pidx, kbs, kt_chunk, Vsrc, q_blk0, head_ids, evac):
            if kind == "C":
                accs = [4, 5, 6, 7]
            else:
                a0 = 4 + 2 * nxt("acc", 2)
                accs = [a0, a0 + 1]
            groups = sorted(set(kb // 8 for kb in kbs))
            ktslot, vslot = {}, {}
            W = 192 if kind == "C" else 384

            def ensure(gi):
                if gi >= len(groups) or groups[gi] in ktslot:
                    return
                g = groups[gi]
                k0 = max(g * 8, min(kbs))
                k1 = min(g * 8 + 8, max(kbs) + 1)
                ks = nxt("kt", 3)
                vs = nxt("v", 3)
                P.dma("sp", kt_t[ks][:, (k0 - g * 8) * 128:(k1 - g * 8) * 128],
                      uT[kt_chunk, :, k0 * 128:k1 * 128], [], [b_kt[ks]])
                P.dma("sp", v_t[vs][:, k0 - g * 8:k1 - g * 8, 0:W], Vsrc(g)[:, k0 - g * 8:k1 - g * 8, :], [], [b_v[vs]])
                ktslot[g], vslot[g] = ks, vs

            ensure(0)
            ensure(1)
            nk = len(kbs)
            pend = []

            def issue_qk(i):
                kb = kbs[i]
                g = kb // 8
                ko = (kb - g * 8) * 128
                ss = nxt("st", 2)
                for e in range(2):
                    P.mm(bank[2 * ss + e], kt_t[ktslot[g]][e * 64:(e + 1) * 64, ko:ko + 128],
                         qt_t[qsl][e * 64:(e + 1) * 64, qi, :], True, True,
                         [b_kt[ktslot[g]], b_qt[qsl]], [b_bank[2 * ss + e]], tp=(e * 64, 0))
                return ss

            def issue_exp(i, ss):
                kb = kbs[i]
                j = kb - q_blk0
                if view:
                    j = ((j + NKB // 2) % NKB) - NKB // 2
                ps_ = nxt("pt", 3)
                src_b = [b_bank[2 * ss], b_bank[2 * ss + 1]]
                if kind == "D":
                    P.act(pt_t[ps_][:], st_ps[ss][:], AF.Exp, src_b, [b_pt[ps_]])
                elif kind == "C" and (j < -1 or j > 4):
                    side = 0 if j < -1 else 1
                    h = head_ids[0]
                    if view and kb >= q_blk0:
                        P.act(pt_t[ps_][:], st_ps[ss][:], AF.Exp, src_b + [b_view], [b_pt[ps_]],
                              bias=farbC[:, kb, h:h + 1])
                    else:
                        P.act(pt_t[ps_][:], st_ps[ss][:], AF.Exp, src_b + [b_const], [b_pt[ps_]],
                              bias=farb[:, side, 8 + h:9 + h])
                else:
                    tm = nxt("tmp", 2)
                    h = head_ids[0]
                    s0 = (4 - j) * 128
                    in1 = stripsC[:, h, s0:s0 + 512]
                    xr = [b_const]
                    if view and ((qt == 0 and j == -1) or (qt == NQR - 1 and j == 4)):
                        lohi = 0 if j == -1 else 1
                        P.ts("dve", tmpf[:], stripsC[:, h, s0:s0 + 512], flg[:, lohi:lohi + 1],
                             cseam[:, 4 * lohi + h:4 * lohi + h + 1], ALU.mult, ALU.add, [b_const, b_view], [b_tmpf])
                        in1 = tmpf[:]
                        xr = [b_tmpf]
                    for e in range(2):
                        P.tt("dve", tmp[tm][:, e * 512:(e + 1) * 512], st_ps[ss][:, e * 512:(e + 1) * 512],
                             in1, ALU.add, [b_bank[2 * ss + e]] + xr, [b_tmp2[tm][e]])
                    P.act(pt_t[ps_][:], tmp[tm][:], AF.Exp, b_tmp2[tm], [b_pt[ps_]])
                return ps_

            def issue_pv(i, ps_):
                kb = kbs[i]
                g = kb // 8
                kbi = kb - g * 8
                vt = v_t[vslot[g]]
                first, last = (i == 0), (i == nk - 1)
                if i == 0 or kbs[i - 1] // 8 != g:
                    ensure(groups.index(g) + 2)
                for e in range(2):
                    rhs = pt_t[ps_][:, e * 512:(e + 1) * 512]
                    if kind == "C":
                        P.mm(bank[accs[2 * e]], vt[:, kbi, 0:128], rhs, first, last,
                             [b_v[vslot[g]], b_pt[ps_]], [b_bank[accs[2 * e]]])
                        P.mm(bank[accs[2 * e + 1]], onesb[:], rhs, first, last,
                             [b_const, b_pt[ps_]], [b_bank[accs[2 * e + 1]]])
                    else:
                        vo = e * 192 + (64 if (pidx % 2) else 0)
                        P.mm(bank[accs[e]], vt[:, kbi, vo:vo + 128], rhs, first, last,
                             [b_v[vslot[g]], b_pt[ps_]], [b_bank[accs[e]]])

            ssq_ = [issue_qk(0)]
            if nk > 1:
                ssq_.append(issue_qk(1))
            for i in range(nk):
                ps_ = issue_exp(i, ssq_[i])
                if CFG.get("LOOK2", True):
                    if i + 2 < nk:
                        ssq_.append(issue_qk(i + 2))
                    issue_pv(i, ps_)
                else:
                    issue_pv(i, ps_)
                    if i + 2 < nk:
                        ssq_.append(issue_qk(i + 2))
            evac(accs)

        for qt in range(NQR):
            q0 = qt * 512
            qb0 = qt * 4
            qsl = nxt("qt", 2)
            gsl = nxt("gt", 2)
            P.dma("sp", qt_t[qsl][:], uT[C_AQ:C_AQ + 4, :, q0:q0 + 512].rearrange("c p t -> p c t"), [], [b_qt[qsl]])
            P.dma("sp", gt_t[gsl][:], uT[C_AG:C_AG + 4, :, q0:q0 + 512].rearrange("c p t -> p c t"), [], [b_gt[gsl]])
            if view:
                kbsA = [(qb0 - 1) % NKB] + list(range(qb0, qb0 + 5))
            else:
                kbsA = [kb for kb in range(qb0 - 1, qb0 + 5) if 0 <= kb < NKB]
            ksA = nxt("kt", 3)
            vsA = nxt("v", 3)
            runs = []
            for ii, kb in enumerate(kbsA):
                if runs and kbsA[ii - 1] + 1 == kb and kb // 8 == kbsA[ii - 1] // 8:
                    runs[-1][2] += 1
                else:
                    runs.append([ii, kb, 1])
            for ii, kb, n in runs:
                g = kb // 8
                P.dma("sp", kt_t[ksA][:, ii * 128:(ii + n) * 128], uT[C_AK, :, kb * 128:(kb + n) * 128], [], [b_kt[ksA]])
                P.dma("sp", v_t[vsA][:, ii:ii + n, :], VA[g, :, kb - g * 8:kb - g * 8 + n, :], [], [b_v[vsA]])
            for ob in range(4):
                qb = qb0 + ob
                dls = [dl for dl in (-1, 0, 1) if view or 0 <= qb + dl < NKB]
                pts = []
                for dl in dls:
                    kl = kbsA.index((qb + dl) % NKB)
                    ss = nxt("st", 2)
                    for jj in range(4):
                        for e in range(2):
                            hh = jj + 4 * e
                            P.mm(st_ps[ss][:, hh * 128:(hh + 1) * 128],
                                 kt_t[ksA][e * 64:(e + 1) * 64, kl * 128:(kl + 1) * 128],
                                 qt_t[qsl][e * 64:(e + 1) * 64, jj, ob * 128:(ob + 1) * 128], True, True,
                                 [b_kt[ksA], b_qt[qsl]], [b_bank[2 * ss + e]], tp=(e * 64, 0))
                    tm = nxt("tmp", 2)
                    P.tt("dve", tmp[tm][:].rearrange("p (h q) -> p h q", h=8),
                         st_ps[ss][:].rearrange("p (h q) -> p h q", h=8), strips[:, 0:8, dl + 1, :], ALU.add,
                         [b_bank[2 * ss], b_bank[2 * ss + 1], b_const], b_tmp2[tm])
                    ps_ = nxt("pt", 3)
                    if view and qt == 0 and ob == 0 and dl == -1:
                        P.act(pt_t[ps_][:], tmp[tm][:], AF.Exp, b_tmp2[tm] + [b_view], [b_pt[ps_]], bias=flg[:, 2:3])
                    elif view and qt == NQR - 1 and ob == 3 and dl == 1:
                        P.act(pt_t[ps_][:], tmp[tm][:], AF.Exp, b_tmp2[tm] + [b_view], [b_pt[ps_]], bias=flg[:, 3:4])
                    else:
                        P.act(pt_t[ps_][:], tmp[tm][:], AF.Exp, b_tmp2[tm], [b_pt[ps_]])
                    pts.append((ps_, kl))
                ai = nxt("acc", 2)
                acc = ac_ps[ai]
                for hh in range(8):
                    jj, e = hh % 4, hh // 4
                    vo = e * 192 + (64 if (jj % 2) else 0)
                    for ii, (ps_, kl) in enumerate(pts):
                        P.mm(acc[:, hh * 128:(hh + 1) * 128], v_t[vsA][:, kl, vo:vo + 128],
                             pt_t[ps_][:, hh * 128:(hh + 1) * 128], ii == 0, ii == len(pts) - 1,
                             [b_v[vsA], b_pt[ps_]], [b_bank[4 + 2 * ai + e]])
                accb = [b_bank[4 + 2 * ai], b_bank[5 + 2 * ai]]
                cols = slice(ob * 128, (ob + 1) * 128)
                for par in range(2):
                    orow = slice(64, 128) if par else slice(0, 64)
                    srow = slice(0, 64) if par else slice(64, 128)
                    vw = lambda ap: ap.rearrange("p (e j r q) -> p e j r q", e=2, j=2, r=2)[:, :, :, par, :]
                    si = nxt("sc", 3)
                    scv = sc[si][orow, :].rearrange("p (e j q) -> p e j q", e=2, j=2)
                    P.cp("dve", scv, vw(acc[srow, :]), accb, [b_sc[si]])
                    P.tt("dve", scv, scv, vw(esink_bc[orow, :, :].rearrange("p h q -> p (h q)")), ALU.add,
                         [b_sc[si], b_const], [b_sc[si]])
                    P.recip(sc[si][orow, :], sc[si][orow, :], [b_sc[si]], [b_sc[si]])
                    P.tt("dve", scv, vw(acc[orow, :]), scv, ALU.mult, accb + [b_sc[si]], [b_sc[si]])
                    P.tt("dve", bigT[orow, 0:4, cols].rearrange("p (e j) q -> p e j q", e=2), scv,
                         gt_t[gsl][orow, 0:4, cols].rearrange("p (e j) q -> p e j q", e=2), ALU.mult,
                         [b_sc[si], b_gt[gsl]], b_big[0:4])

            if CFG.get("P2STOP") == 1:
                continue
            gslB = nxt("gt", 2)
            P.dma("sp", gt_t[gslB][:], uT[C_BG:C_BG + 4, :, q0:q0 + 512].rearrange("c p t -> p c t"), [], [b_gt[gslB]])
            lo = max(qb0 - 1, 0)
            hi = min(qb0 + 5, NKB)
            P.dma("sp", bxs[:, lo - (qb0 - 1):hi - (qb0 - 1), :],
                  BX[lo * 128:hi * 128, :].rearrange("(b p) c -> p b c", p=128), [], [b_bxs])
            if view and qb0 == 0:
                P.dma("sp", bxs[:, 0, :], BX[(NKB - 1) * 128:NKB * 128, :], [], [b_bxs])
            for g in range(4):
                pb = 4 + (g % 2) * 2
                for ob in range(4):
                    qb = qb0 + ob
                    dls = [dl for dl in (-1, 0, 1) if view or 0 <= qb + dl < NKB]
                    for ii, dl in enumerate(dls):
                        if dl == -1:
                            mi = 0
                        elif dl == 1:
                            mi = 2
                        else:
                            mi = 3 if qb == 0 else (4 if qb == NKB - 1 else 1)
                        mt = mmat[:, g * 5 + mi, :]
                        xr = [b_const]
                        if view and qb == 0 and dl in (-1, 0):
                            mt = mmv[:, g * 4 + (0 if dl == -1 else 1), :]
                            xr = [b_view]
                        elif view and qb == 4 * NQR - 1 and dl in (0, 1):
                            mt = mmv[:, g * 4 + (2 if dl == 0 else 3), :]
                            xr = [b_view]
                        elif view and dl == 0:
                            mt = mmat[:, g * 5 + 1, :]
                        P.mm(bank[pb][:, ob * 128:(ob + 1) * 128], bxs[:, ob + dl + 1, g * 128:(g + 1) * 128],
                             mt, ii == 0, ii == len(dls) - 1, [b_bxs] + xr, [b_bank[pb]])
                pl = 0
                P.cp("act", pooled[pl][:], bank[pb], [b_bank[pb]], [b_pooled[pl]])
                P.mm(bank[pb + 1], poolw_b[:, g, :], pooled[pl][:], True, True, [b_pooled[pl], b_const], [b_bank[pb + 1]])
                P.stt(bigT[:, 4 + g, :], bank[pb + 1], pscale_t[:, g:g + 1], gt_t[gslB][:, g, :], ALU.mult, ALU.mult,
                      [b_bank[pb + 1], b_gt[gslB], b_const], [b_big[4 + g]])

            if CFG.get("P2STOP") == 2:
                continue
            def evac_AD(kind, j, gsl_):
                def f(accs):
                    for e in range(2):
                        h = j + 4 * e
                        mchunk = (h // 2) if kind == "A" else 12 + (h // 2)
                        gidx = h // 2
                        odd = j % 2
                        orow = slice(64, 128) if odd else slice(0, 64)
                        srow = slice(0, 64) if odd else slice(64, 128)
                        a = accs[e]
                        si = nxt("sc", 3)
                        if kind == "A":
                            P.cp("dve", sc[si][orow, :], bank[a][srow, :], [b_bank[a]], [b_sc[si]])
                            P.ts("dve", sc[si][orow, :], sc[si][orow, :], esink[orow, h:h + 1], None, ALU.add, None,
                                 [b_sc[si], b_const], [b_sc[si]])
                            P.recip(sc[si][orow, :], sc[si][orow, :], [b_sc[si]], [b_sc[si]])
                        else:
                            P.recip(sc[si][orow, :], bank[a][srow, :], [b_bank[a]], [b_sc[si]])
                        P.tt("dve", sc[si][orow, :], bank[a][orow, :], sc[si][orow, :], ALU.mult,
                             [b_bank[a], b_sc[si]], [b_sc[si]])
                        P.tt("dve", bigT[orow, mchunk, :], sc[si][orow, :], gt_t[gsl_][orow, gidx, :], ALU.mult,
                             [b_sc[si], b_gt[gsl_]], [b_big[mchunk]])
                return f

            qsl = nxt("qt", 2)
            gsl = nxt("gt", 2)
            P.dma("sp", qt_t[qsl][:], uT[C_DQ:C_DQ + 4, :, q0:q0 + 512].rearrange("c p t -> p c t"), [], [b_qt[qsl]])
            P.dma("sp", gt_t[gsl][:], uT[C_DG:C_DG + 4, :, q0:q0 + 512].rearrange("c p t -> p c t"), [], [b_gt[gsl]])
            allkb = list(range(NKB))
            for j in range(4):
                attention("D", qsl, j, j, allkb, C_DK, lambda g: VD[g, :, :, :], qb0, (j, j + 4),
                          evac_AD("D", j, gsl))

            if CFG.get("P2STOP") == 3:
                continue
            qsl = nxt("qt", 2)
            gsl = nxt("gt", 2)
            P.dma("sp", qt_t[qsl][:], uT[C_CQ:C_CQ + 4, :, q0:q0 + 512].rearrange("c p t -> p c t"), [], [b_qt[qsl]])
            P.dma("sp", gt_t[gsl][:], uT[C_CG:C_CG + 4, :, q0:q0 + 512].rearrange("c p t -> p c t"), [], [b_gt[gsl]])

            def evac_C(h, gsl_):
                def f(accs):
                    R, T = tmp[0], tmp[1]
                    for e in range(2):
                        ao, asum = accs[2 * e], accs[2 * e + 1]
                        P.recip(R[:, e * 512:(e + 1) * 512], bank[asum], [b_bank[asum]], [b_tmp2[0][e]])
                        P.tt("dve", T[:, e * 512:(e + 1) * 512], bank[ao], R[:, e * 512:(e + 1) * 512],
                             ALU.mult, [b_bank[ao], b_tmp2[0][e]], [b_tmp2[1][e]])
                    P.stt(T[:, 0:512], T[:, 512:1024], neglam[:, 0:1], T[:, 0:512], ALU.mult, ALU.add,
                          [b_tmp2[1][0], b_tmp2[1][1], b_const], [b_tmp2[1][0], b_tmp2[1][1]])
                    P.act(ysq[:], T[:, 0:512], AF.Square, [b_tmp2[1][0], b_tmp2[1][1]], [b_ysq])
                    sb_ = accs[0]
                    P.mm(bank[sb_], onesb[:], ysq[:], True, True, [b_ysq, b_const], [b_bank[sb_]])
                    s1 = nxt("sc", 3)
                    P.act(sc[s1][:], bank[sb_], AF.Sqrt, [b_bank[sb_]], [b_sc[s1]], scale=1.0 / 128, bias=EPS)
                    P.recip(sc[s1][:], sc[s1][:], [b_sc[s1]], [b_sc[s1]])
                    P.stt(T[:, 0:512], T[:, 0:512], subg[:, 0:1], sc[s1][:], ALU.mult, ALU.mult,
                          [b_tmp2[1][0], b_tmp2[1][1], b_sc[s1], b_const], [b_tmp2[1][0], b_tmp2[1][1]])
                    P.tt("dve", bigT[:, 8 + h, :], T[:, 0:512], gt_t[gsl_][:, h, :], ALU.mult,
                         [b_tmp2[1][0], b_tmp2[1][1], b_gt[gsl_]], [b_big[8 + h]])
                return f

            for h in range(4):
                attention("C", qsl, h, 0, allkb, C_CK + h, (lambda g, h=h: VC[h, g, :, :, :]), qb0, (h, h),
                          evac_C(h, gsl))

            if CFG.get("P2STOP") == 4:
                continue
            if view:
                P.dma("sp", pef[:], E["peT_s2"][:, q0:q0 + 512].rearrange("(k p) t -> p k t", p=128), [], [b_pe])
            else:
                P.dma("sp", pef[:], peT[l, :, q0:q0 + 512].rearrange("(k p) t -> p k t", p=128), [], [b_pe])
            P.cp("act", peb[:], pef[:], [b_pe], [b_pe])
            for ch in range(2):
                for kc in range(NKC):
                    ws = nxt("wos", NWOS)
                    P.dma("sp", wos[ws][:], wo_s[l, kc, :, ch * 1024:(ch + 1) * 1024], [], [b_wos[ws]])
                    for tb in range(4):
                        for cb in range(2):
                            bi = tb * 2 + cb
                            P.mm(bank[bi], bigT[:, kc, tb * 128:(tb + 1) * 128], wos[ws][:, cb * 512:(cb + 1) * 512],
                                 kc == 0, kc == NKC - 1, [b_big[kc], b_wos[ws]], [b_bank[bi]])
                for tb in range(4):
                    for cb in range(2):
                        bi = tb * 2 + cb
                        c0 = ch * 1024 + cb * 512
                        P.cp("dve" if cb == 0 else "act", osb[tb][:, c0:c0 + 512], bank[bi], [b_bank[bi]], [b_osb[tb]])
            for tb in range(4):
                P.act(x1b[:], osb[tb][:], AF.Square, [b_osb[tb]], [b_x1b, b_ss], accum_out=ssr[:, tb:tb + 1])
            P.act(sdr[:], ssr[:], AF.Sqrt, [b_ss], [b_ss], scale=1.0 / D, bias=EPS)
            P.recip(rsr[:], sdr[:], [b_ss], [b_ss])
            if CFG.get("P2STOP") == 42:
                continue
            for tb in range(4):
                r0 = q0 + tb * 128
                if view:
                    bi_ = r0 // 128
                    P.op("pool", lambda h, bi_=bi_: h.indirect_dma_start(
                        out=xres[:], out_offset=None, in_=xsrc[:, :],
                        in_offset=bass.IndirectOffsetOnAxis(ap=idxt[:, bi_:bi_ + 1], axis=0)),
                        [b_view], [b_xres], dma=True)
                else:
                    P.dma("sp", xres[:], xsrc[r0:r0 + 128, :], [], [b_xres])
                P.stt(osb[tb][:], osb[tb][:], rsr[:, tb:tb + 1], gpost_bc[:], ALU.mult, ALU.mult,
                      [b_osb[tb], b_ss, b_const], [b_osb[tb]])
                P.tt("dve", osb[tb][:], osb[tb][:], xres[:], ALU.add, [b_osb[tb], b_xres], [b_osb[tb]])
                if CFG.get("P2STOP") == 43:
                    continue
                P.cp("act", x1b[:], osb[tb][:], [b_osb[tb]], [b_x1b])
                for g4 in range(4):
                    bi = (tb * 4 + g4) % 8
                    for i in range(4):
                        kc = g4 * 4 + i
                        P.mm(bank[bi][:, i * 128:(i + 1) * 128], x1b[:, kc * 128:(kc + 1) * 128], identb[:], True, True,
                             [b_x1b, b_const], [b_bank[bi]])
                    P.cp("act" if g4 % 2 == 0 else "dve", bigT[:, g4 * 4:(g4 + 1) * 4, tb * 128:(tb + 1) * 128],
                         bank[bi].rearrange("p (k t) -> p k t", k=4), [b_bank[bi]], b_big[g4 * 4:(g4 + 1) * 4])
            if CFG.get("P2STOP") == 5:
                continue
            for cq in range(4):
                c0 = cq * 512
                for kc in range(NKC):
                    ws = nxt("wps", NWPS)
                    P.dma("sp", wps[ws][:], wpg_s[l, kc, :, c0:c0 + 512], [], [b_wps[ws]])
                    for tb in range(4):
                        P.mm(bank[tb], bigT[:, kc, tb * 128:(tb + 1) * 128], wps[ws][:], kc == 0, kc == NKC - 1,
                             [b_big[kc], b_wps[ws]], [b_bank[tb]])
                for tb in range(4):
                    for k2 in range(2):
                        P.mm(bank[4 + tb], peb[:, k2, tb * 128:(tb + 1) * 128], wpe_b[:, k2, c0:c0 + 512], k2 == 0, k2 == 1,
                             [b_pe, b_const], [b_bank[4 + tb]])
                for tb in range(4):
                    si = nxt("sc", 3)
                    P.act(sc[si][:], bank[tb], AF.Sigmoid, [b_bank[tb]], [b_sc[si]])
                    P.tt("dve", tmpf[:], bank[4 + tb], sc[si][:], ALU.mult, [b_bank[4 + tb], b_sc[si]], [b_tmpf])
                    P.tt("dve", osb[tb][:, c0:c0 + 512], osb[tb][:, c0:c0 + 512], tmpf[:], ALU.add,
                         [b_osb[tb], b_tmpf], [b_osb[tb]])
            for tb in range(4):
                r0 = q0 + tb * 128
                P.dma("pool", xdst[r0:r0 + 128, :], osb[tb][:], [b_osb[tb]], [])
        P.flush()


_PROG_CACHE = {}


def _prep_inputs(inp, SP, SS, DEPTH, ncores):
    cols = dev_cols()
    f = lambda a: np.ascontiguousarray(np.asarray(a, dtype=np.float32))
    shared = {
        "w_in": f(np.asarray(inp["w_in"])[:, :, cols]),
        "w_o": f(inp["w_o"]),
        "w_pg": f(inp["w_pg"]),
        "w_pe": f(inp["w_pe"]),
        "gpreT": f(np.asarray(inp["g_pre"]).reshape(DEPTH, 16, 128).transpose(0, 2, 1)),
        "g_post": f(inp["g_post"]),
        "sink": f(inp["sink_a"]),
        "pool_w": f(inp["pool_w"]),
        "pscaleT": f(np.asarray(inp["pool_scale"]).reshape(DEPTH, 4, 128).transpose(0, 2, 1)),
        "lam4": f(np.stack([np.asarray(inp[k]) for k in ("lam_q1", "lam_k1", "lam_q2", "lam_k2")], axis=1)),
        "subln": f(np.asarray(inp["diff_subln"]).reshape(DEPTH, 128, 1)),
        "qn128": f(np.tile(np.asarray(inp["qnorm_d"]), (1, 2)).reshape(DEPTH, 128, 1)),
        "kn128": f(np.tile(np.asarray(inp["knorm_d"]), (1, 2)).reshape(DEPTH, 128, 1)),
        "relb": f(inp["rel_bias"]),
    }
    shared.update(host_consts(max(SP, SS)))
    xp = np.asarray(inp["x_prompt"])
    xs = np.asarray(inp["x_sample"])
    pp = np.asarray(inp["p_prompt"])
    psm = np.asarray(inp["p_sample"])
    nb_p, nb_s = xp.shape[0], xs.shape[0]
    nkb = SS // 128
    q4 = SS // 4
    mm = np.asarray(shared["mmat"]).astype(np.float32)
    maps = []
    for i in range(ncores):
        m = dict(shared)
        b = i % nb_s
        r = (i // nb_s) % 4
        m["x_p"] = f(xp[i % nb_p])
        m["x_s"] = f(xs[b])
        m["peT_p"] = f(pp[:, i % nb_p].transpose(0, 2, 1))
        m["peT_s"] = f(psm[:, b].transpose(0, 2, 1))
        vtok = (np.arange(SS) + r * q4) % SS
        m["rowidx"] = np.ascontiguousarray(vtok.reshape(nkb, 128).T.astype(np.int32))
        m["isr"] = ((np.arange(nkb) + r * (nkb // 4)) < nkb).astype(np.float32).reshape(1, nkb)
        m["flags"] = np.array([[1.0 if r > 0 else 0.0, 1.0 if r < 3 else 0.0,
                                0.0 if r > 0 else -BIG, 0.0 if r < 3 else -BIG]], np.float32)
        mv = np.zeros((128, 16, 128), np.float32)
        for g in range(4):
            mv[:, g * 4 + 0] = mm[:, g * 5 + 0] if r > 0 else 0.0
            mv[:, g * 4 + 1] = mm[:, g * 5 + 1] if r > 0 else mm[:, g * 5 + 3]
            mv[:, g * 4 + 2] = mm[:, g * 5 + 1] if r < 3 else mm[:, g * 5 + 4]
            mv[:, g * 4 + 3] = mm[:, g * 5 + 2] if r < 3 else 0.0
        m["mmatv"] = mv.astype(ml_dtypes.bfloat16)
        m["cosT2"] = np.ascontiguousarray(shared["cosT"][:, :SS][:, vtok])
        m["sinT2"] = np.ascontiguousarray(shared["sinT"][:, :SS][:, vtok])
        m["peT_s2"] = f(psm[DEPTH - 1, b, r * q4:(r + 1) * q4, :].T)
        maps.append(m)
    return maps


def kernel(**inp):
    SP, SS, DEPTH, ncores = CFG["SP"], CFG["SS"], CFG["DEPTH"], CFG["NCORES"]
    key = (SP, SS, DEPTH)
    maps = _prep_inputs(inp, SP, SS, DEPTH, ncores)
    nc = build_program(SP, SS, DEPTH)
    res = run_bass_kernel_spmd(nc, maps, core_ids=list(range(ncores)))
    nb_p = np.asarray(inp["x_prompt"]).shape[0]
    nb_s = np.asarray(inp["x_sample"]).shape[0]
    y_p = np.stack([np.asarray(res.results[i]["y_p"], dtype=np.float32) for i in range(nb_p)], 0)
    q4 = SS // 4
    y_s = np.zeros((nb_s, SS, D), np.float32)
    for i in range(ncores):
        b, r = i % nb_s, (i // nb_s) % 4
        if i // nb_s < 4:
            y_s[b, r * q4:(r + 1) * q4] = np.asarray(res.results[i]["y_s"], dtype=np.float32)
    return (y_p, y_s)
```

```python
import math
from contextlib import ExitStack

import numpy as np
import ml_dtypes

import concourse.bass as bass
import concourse.mybir as mybir
from concourse.bass_utils import run_bass_kernel_spmd

F32 = mybir.dt.float32
BF16 = mybir.dt.bfloat16
ALU = mybir.AluOpType
AF = mybir.ActivationFunctionType
AX = mybir.AxisListType

D = 2048
NKC = 16
INW = 5632
PLE = 256
EPS = 1e-6
BIG = 30000.0
CFG = {"SP": 4096, "SS": 8192, "DEPTH": 2, "NCORES": 8, "STOP": None}

C_AQ, C_AK, C_AG, C_BG, C_CQ, C_CK, C_CG, C_DQ, C_DK, C_DG = 0, 4, 5, 9, 13, 17, 21, 25, 29, 30
NFM = 34


def dev_cols():
    aq, ak, av, ag, bx, bg, cq, ck, cv, cg, dq, dk, dv, dg = (0, 512, 640, 768, 1280, 1792, 2304, 2816, 3328,
                                                               3840, 4352, 4864, 4992, 5120)
    cols = []
    for j in range(4):
        cols += list(range(aq + j * 64, aq + j * 64 + 64)) + list(range(aq + (4 + j) * 64, aq + (5 + j) * 64))
    cols += list(range(ak, ak + 128))
    cols += list(range(ag, ag + 512))
    cols += list(range(bg, bg + 512))
    cols += list(range(cq, cq + 512))
    cols += list(range(ck, ck + 512))
    cols += list(range(cg, cg + 512))
    for j in range(4):
        cols += list(range(dq + j * 64, dq + j * 64 + 64)) + list(range(dq + (4 + j) * 64, dq + (5 + j) * 64))
    cols += list(range(dk, dk + 128))
    cols += list(range(dg, dg + 512))
    assert len(cols) == NFM * 128
    cols += list(range(av, av + 128)) + list(range(dv, dv + 128)) + list(range(cv, cv + 512)) + list(range(bx, bx + 512))
    assert len(cols) == INW and len(set(cols)) == INW
    return np.array(cols)


def rel_bucket_np(rel):
    nb, max_exact = 16, 8
    ret = np.where(rel > 0, nb, 0)
    n = np.abs(rel)
    nf = np.maximum(n, 1).astype(np.float32)
    large = max_exact + (np.log(nf / np.float32(max_exact)) / np.float32(math.log(128 / max_exact))
                         * np.float32(nb - max_exact)).astype(np.int32)
    large = np.minimum(large, nb - 1)
    return ret + np.where(n < max_exact, n, large)


def host_consts(smax):
    c = {}
    rows = smax // 64
    t_row = np.repeat(np.arange(rows), 64).astype(np.float32)
    t_col = np.tile(np.arange(64), rows).astype(np.float32)
    inv = (np.float32(10000.0) ** (-np.arange(0, 32, 2, dtype=np.float32) / np.float32(32))).astype(np.float32)
    ang_r = t_row[:, None] * inv[None, :]
    ang_c = t_col[:, None] * inv[None, :]
    ang = np.concatenate([ang_r, ang_r, ang_c, ang_c], axis=-1).astype(np.float32)
    cos = np.cos(ang).astype(np.float32).T
    sin = np.sin(ang).astype(np.float32).T
    c["cosT"] = np.ascontiguousarray(np.concatenate([cos, cos], 0))
    c["sinT"] = np.ascontiguousarray(np.concatenate([sin, sin], 0))
    rel = np.arange(-255, 256)
    bk = rel_bucket_np(rel)
    erow = np.zeros((33, 511), np.float32)
    erow[bk, np.arange(511)] = 1.0
    erow[32] = (np.abs(rel) > 128).astype(np.float32)
    c["erow"] = erow
    c["maskrow"] = np.array([[-BIG] * 8 + [0.0] * 4], np.float32)
    r64 = np.zeros((64, 64), np.float32)
    for base in (0, 32):
        for i in range(16):
            r64[base + 16 + i, base + i] = -1.0
            r64[base + i, base + 16 + i] = 1.0
    rm = np.zeros((128, 128), np.float32)
    rm[:64, :64] = r64
    rm[64:, 64:] = r64
    c["rmat"] = rm
    c["identf"] = np.eye(128, dtype=np.float32)
    c["identb"] = np.eye(128).astype(ml_dtypes.bfloat16)
    c["onesb"] = np.ones((128, 128)).astype(ml_dtypes.bfloat16)
    bd = np.zeros((128, 128), np.float32)
    bd[:64, :64] = 1.0
    bd[64:, 64:] = 1.0
    c["bdb"] = bd.astype(ml_dtypes.bfloat16)
    s5 = 5 * 128
    mm = np.zeros((128, 20, 128), np.float32)
    t = np.arange(s5)
    for gi, size in enumerate((2, 4, 8, 16)):
        lo = size // 2
        hi = size - lo - 1
        start = np.maximum(t - lo, 0)
        end = np.minimum(t + hi, s5 - 1) + 1
        w = np.zeros((s5, s5), np.float64)
        for ti in range(s5):
            w[ti, start[ti]:end[ti]] = 1.0 / (end[ti] - start[ti])
            w[ti, ti] -= 1.0
        blk = lambda a, b: w[a * 128:(a + 1) * 128, b * 128:(b + 1) * 128].T
        mm[:, gi * 5 + 0, :] = blk(2, 1)
        mm[:, gi * 5 + 1, :] = blk(2, 2)
        mm[:, gi * 5 + 2, :] = blk(2, 3)
        mm[:, gi * 5 + 3, :] = blk(0, 0)
        mm[:, gi * 5 + 4, :] = blk(4, 4)
    c["mmat"] = mm.astype(ml_dtypes.bfloat16)
    return c


class Buf:
    __slots__ = ("w", "r", "name")

    def __init__(self, name=""):
        self.w = None
        self.r = {}
        self.name = name


class Op:
    __slots__ = ("eng", "fn", "deps", "inc", "sem", "semval", "dma", "key")

    def __init__(self, eng, fn, dma):
        self.eng, self.fn, self.dma = eng, fn, dma
        self.inc = dma
        self.sem = None
        self.semval = 0
        self.deps = ()
        self.key = None


class Prog:
    ENGS = ("pe", "act", "dve", "pool", "sp")

    def __init__(self, nc, es):
        self.nc = nc
        self.h = {"pe": nc.tensor, "act": nc.scalar, "dve": nc.vector, "pool": nc.gpsimd, "sp": nc.sync}
        self.csem = {}
        self.cval = {}
        for e in ("pe", "act", "dve", "pool"):
            self.csem[e] = es.enter_context(nc.semaphore("c_" + e))
            self.cval[e] = 0
        self.dsem = {"sp": [], "pool": [], "act": []}
        for q, n in (("sp", 12), ("pool", 8), ("act", 4)):
            for i in range(n):
                self.dsem[q].append(es.enter_context(nc.semaphore("d_%s%d" % (q, i))))
        self.dval = {}
        self.dlast = {}
        self.drr = {"sp": 0, "pool": 0, "act": 0}
        self.seen = {e: {} for e in self.ENGS}
        self.ops = {e: [] for e in self.ENGS}
        self.bufs = []
        self.nops = 0

    def buf(self, name=""):
        b = Buf(name)
        self.bufs.append(b)
        return b

    def op(self, eng, fn, reads=(), writes=(), dma=False):
        o = Op(eng, fn, dma)
        deps = {}
        for b in reads:
            if b.w is not None:
                deps[id(b.w)] = b.w
        for b in writes:
            if b.w is not None:
                deps[id(b.w)] = b.w
            for r in b.r.values():
                deps[id(r)] = r
        if dma:
            lst = self.dsem[eng]
            k = self.drr[eng]
            self.drr[eng] = (k + 1) % len(lst)
            s = lst[k]
            o.sem = s
            o.key = ("d", eng, k)
            prev = self.dlast.get(o.key)
            if prev is not None:
                deps[id(prev)] = prev
            self.dlast[o.key] = o
            self.dval[o.key] = self.dval.get(o.key, 0) + 16
            o.semval = self.dval[o.key]
        else:
            o.sem = self.csem[eng]
            o.key = ("c", eng)
        deps.pop(id(o), None)
        o.deps = tuple(deps.values())
        rk = (eng, id(o)) if dma else eng
        for b in reads:
            b.r[rk] = o
        for b in writes:
            b.w = o
            b.r = {}
        self.ops[eng].append(o)
        self.nops += 1
        return o

    def mm(self, out, lhsT, rhs, start, stop, reads, writes, tp=None):
        if tp is None:
            f = lambda h: h.matmul(out, lhsT=lhsT, rhs=rhs, start=start, stop=stop)
        else:
            f = lambda h: h.matmul(out, lhsT=lhsT, rhs=rhs, start=start, stop=stop, tile_position=tp)
        return self.op("pe", f, reads, writes)

    def tr(self, out, in_, ident, reads, writes):
        return self.op("pe", lambda h: h.transpose(out, in_, ident), reads, writes)

    def act(self, out, in_, func, reads, writes, **kw):
        return self.op("act", lambda h: h.activation(out=out, in_=in_, func=func, **kw), reads, writes)

    def dma(self, q, out, in_, reads, writes):
        return self.op(q, lambda h: h.dma_start(out=out, in_=in_), reads, writes, dma=True)

    def tt(self, eng, out, in0, in1, op, reads, writes):
        return self.op(eng, lambda h: h.tensor_tensor(out=out, in0=in0, in1=in1, op=op), reads, writes)

    def ts(self, eng, out, in0, s1, s2, op0, op1, reads, writes):
        if op1 is None:
            f = lambda h: h.tensor_scalar(out=out, in0=in0, scalar1=s1, scalar2=None, op0=op0)
        else:
            f = lambda h: h.tensor_scalar(out=out, in0=in0, scalar1=s1, scalar2=s2, op0=op0, op1=op1)
        return self.op(eng, f, reads, writes)

    def stt(self, out, in0, scalar, in1, op0, op1, reads, writes, eng="dve"):
        return self.op(eng, lambda h: h.scalar_tensor_tensor(out=out, in0=in0, scalar=scalar, in1=in1,
                                                             op0=op0, op1=op1), reads, writes)

    def cp(self, eng, out, in_, reads, writes):
        if eng == "act":
            return self.act(out, in_, AF.Copy, reads, writes)
        return self.op(eng, lambda h: h.tensor_copy(out=out, in_=in_), reads, writes)

    def recip(self, out, in_, reads, writes):
        return self.op("dve", lambda h: h.reciprocal(out=out, in_=in_), reads, writes)

    def memset(self, eng, ap, val, writes):
        return self.op(eng, lambda h: h.memset(ap, val), (), writes)

    def flush(self):
        nc = self.nc
        for e in self.ENGS:
            for o in self.ops[e]:
                for d in o.deps:
                    if not d.dma:
                        d.inc = True
        for e in ("pe", "act", "dve", "pool"):
            v = self.cval[e]
            for o in self.ops[e]:
                if not o.dma and o.inc:
                    v += 1
                    o.semval = v
            self.cval[e] = v
        prog = self

        def emit(e, h):
            seen = prog.seen[e]
            for o in prog.ops[e]:
                for d in o.deps:
                    if (not d.dma) and d.eng == e and e == "pe":
                        continue
                    if seen.get(d.key, 0) >= d.semval:
                        continue
                    h.wait_ge(d.sem, d.semval)
                    seen[d.key] = d.semval
                ins = o.fn(h)
                if o.inc:
                    ins.then_inc(o.sem, 16 if o.dma else 1)
            for key, last in prog.dlast.items():
                if key[1] == e and seen.get(key, 0) < prog.dval[key]:
                    h.wait_ge(last.sem, prog.dval[key])
                    seen[key] = prog.dval[key]

        with nc.Block() as block:
            if self.ops["sp"]:
                block.sync(lambda h: emit("sp", h))
            if self.ops["pe"]:
                block.tensor(lambda h: emit("pe", h))
            if self.ops["act"]:
                block.scalar(lambda h: emit("act", h))
            if self.ops["dve"]:
                block.vector(lambda h: emit("dve", h))
            if self.ops["pool"]:
                block.gpsimd(lambda h: emit("pool", h))
        self.ops = {e: [] for e in self.ENGS}
        for b in self.bufs:
            b.w = None
            b.r = {}
        self.bufs = []


class TileAlloc:
    def __init__(self, nc, prog):
        self.nc, self.prog = nc, prog
        self.es = ExitStack()

    def __enter__(self):
        self.es.__enter__()
        return self

    def __exit__(self, *a):
        return self.es.__exit__(*a)

    _uid = [0]

    def _nm(self, name):
        TileAlloc._uid[0] += 1
        return "t%d_%s" % (TileAlloc._uid[0], name)

    def sb(self, name, shape, dt):
        return self.es.enter_context(self.nc.sbuf_tensor(self._nm(name), list(shape), dt))

    def ps(self, name, shape, dt):
        return self.es.enter_context(self.nc.psum_tensor(self._nm(name), list(shape), dt))


def build_program(SP, SS, DEPTH):
    nc = bass.Bass("TRN2", target_bir_lowering=False)
    SMAX = max(SP, SS)
    seqs = [("p", SP), ("s", SS)]

    def din(name, shape, dt=F32):
        return nc.dram_tensor(name, list(shape), dt, kind="ExternalInput").ap()

    def dscr(name, shape, dt):
        return nc.dram_tensor(name, list(shape), dt).ap()

    x_in = {"p": din("x_p", [SP, D]), "s": din("x_s", [SS, D])}
    peT_in = {"p": din("peT_p", [DEPTH, PLE, SP]), "s": din("peT_s", [DEPTH, PLE, SS])}
    w_in = din("w_in", [DEPTH, D, INW])
    w_o = din("w_o", [DEPTH, D, D])
    w_pg = din("w_pg", [DEPTH, D, D])
    w_pe = din("w_pe", [DEPTH, PLE, D])
    gpreT = din("gpreT", [DEPTH, 128, 16])
    g_post = din("g_post", [DEPTH, D])
    sink = din("sink", [DEPTH, 8])
    pool_w = din("pool_w", [DEPTH, 4, 128, 128])
    pscaleT = din("pscaleT", [DEPTH, 128, 4])
    lam4 = din("lam4", [DEPTH, 4, 64])
    subln = din("subln", [DEPTH, 128, 1])
    qn128 = din("qn128", [DEPTH, 128, 1])
    kn128 = din("kn128", [DEPTH, 128, 1])
    relb = din("relb", [32, 12])
    cosT = din("cosT", [128, SMAX])
    sinT = din("sinT", [128, SMAX])
    erow_d = din("erow", [33, 511])
    maskrow = din("maskrow", [1, 12])
    rmat_d = din("rmat", [128, 128])
    identf_d = din("identf", [128, 128])
    identb_d = din("identb", [128, 128], BF16)
    onesb_d = din("onesb", [128, 128], BF16)
    bdb_d = din("bdb", [128, 128], BF16)
    mmat_d = din("mmat", [128, 20, 128], BF16)
    rowidx_d = din("rowidx", [128, SS // 128], mybir.dt.int32)
    isr_d = din("isr", [1, SS // 128])
    flags_d = din("flags", [1, 4])
    mmatv_d = din("mmatv", [128, 16, 128], BF16)
    cosT2 = din("cosT2", [128, SS])
    sinT2 = din("sinT2", [128, SS])
    peT_s2 = din("peT_s2", [PLE, SS // 4])
    y_out = {"p": nc.dram_tensor("y_p", [SP, D], F32, kind="ExternalOutput").ap(),
             "s": nc.dram_tensor("y_s", [SS // 4, D], F32, kind="ExternalOutput").ap()}

    wc = dscr("wc", [DEPTH, NFM, 128, 2048], BF16)
    wtm = dscr("wtm", [DEPTH, 5, 128, 16, 256], BF16)
    wo_s = dscr("wo_s", [DEPTH, 16, 128, 2048], BF16)
    wpg_s = dscr("wpg_s", [DEPTH, 16, 128, 2048], BF16)
    uT, VA, VD, VC, BX, x1s = {}, {}, {}, {}, {}, {}
    for k, S in seqs:
        G = S // 1024
        uT[k] = dscr("uT_" + k, [NFM, 128, S], BF16)
        VA[k] = dscr("VA_" + k, [G, 128, 8, 384], BF16)
        VD[k] = dscr("VD_" + k, [G, 128, 8, 384], BF16)
        VC[k] = dscr("VC_" + k, [4, G, 128, 8, 192], BF16)
        BX[k] = dscr("BX_" + k, [S, 512], BF16)
        x1s[k] = [dscr("x1a_" + k, [S, D], F32), dscr("x1b_" + k, [S, D], F32)]

    with ExitStack() as es:
        P = Prog(nc, es)
        pers = TileAlloc(nc, P)
        es.enter_context(pers)
        identb = pers.sb("identb", [128, 128], BF16)
        onesb = pers.sb("onesb", [128, 128], BF16)
        bdb = pers.sb("bdb", [128, 128], BF16)
        rmat = pers.sb("rmat", [128, 128], F32)
        mmat = pers.sb("mmat", [128, 20, 128], BF16)
        strips = pers.sb("strips", [128, 8, 3, 128], F32)
        stripsC = pers.sb("stripsC", [128, 4, 9 * 128], F32)
        esink_bc = pers.sb("esink_bc", [128, 8, 128], F32)
        farb = pers.sb("farb", [128, 2, 12], F32)
        wpe_b = pers.sb("wpe_b", [128, 2, 2048], BF16)
        gpost_bc = pers.sb("gpost_bc", [128, 2048], F32)
        poolw_b = pers.sb("poolw_b", [128, 4, 128], BF16)
        pscale_t = pers.sb("pscale_t", [128, 4], F32)
        gq_t = pers.sb("gq_t", [128, 1], F32)
        gk_t = pers.sb("gk_t", [128, 1], F32)
        esink = pers.sb("esink", [128, 8], F32)
        neglam = pers.sb("neglam", [128, 1], F32)
        subg = pers.sb("subg", [128, 1], F32)

        with TileAlloc(nc, P) as ta:
            erow = ta.sb("erow", [33, 511], F32)
            strips_all = ta.sb("strips_all", [128, 12, 3, 128], F32)
            tabext = ta.sb("tabext", [33, 12], F32)
            psb = [ta.ps("ips%d" % i, [128, 512], F32) for i in range(2)]
            b_c = P.buf()
            for dst, src in ((identb, identb_d), (onesb, onesb_d), (bdb, bdb_d),
                             (rmat, rmat_d), (erow, erow_d)):
                P.dma("sp", dst[:], src[:, :], [], [b_c])
            P.dma("sp", mmat[:], mmat_d[:, :, :], [], [b_c])
            P.dma("sp", tabext[0:32, :], relb[:, :], [], [b_c])
            P.dma("sp", tabext[32:33, :], maskrow[:, :], [], [b_c])
            P.dma("sp", farb[:, 0, :], relb[15:16, :].partition_broadcast(128), [], [b_c])
            P.dma("sp", farb[:, 1, :], relb[31:32, :].partition_broadcast(128), [], [b_c])
            b_str = P.buf()
            bps = [P.buf(), P.buf()]
            nb = 0
            for di in range(3):
                dl = di - 1
                for q0 in range(0, 128, 32):
                    ps = psb[nb % 2]
                    bp = bps[nb % 2]
                    nb += 1
                    for ci in range(32):
                        qq = q0 + ci
                        w0 = 128 * dl - qq + 255
                        P.mm(ps[:, ci * 12:(ci + 1) * 12], erow[0:33, w0:w0 + 128], tabext[0:33, :], True, True,
                             [b_c], [bp])
                    P.cp("dve", strips_all[:, :, di, q0:q0 + 32],
                         ps[:, 0:384].rearrange("p (c h) -> p h c", h=12), [bp], [b_str])
            b_sc9 = P.buf()
            P.cp("dve", strips[:], strips_all[:, 0:8, :, :], [b_str], [b_sc9])
            for h in range(4):
                fin0 = mmat[:, 0:3, :].rearrange("p a b -> p (a b)")
                P.ts("dve", stripsC[:, h, 0:384], fin0, 0.0, farb[:, 1, 8 + h:9 + h], ALU.mult, ALU.add,
                     [b_str, b_c], [b_sc9])
                P.ts("dve", stripsC[:, h, 768:1152], fin0, 0.0, farb[:, 0, 8 + h:9 + h], ALU.mult, ALU.add,
                     [b_str, b_c], [b_sc9])
                for dl in (1, 0, -1):
                    blk = 4 - dl
                    P.cp("dve", stripsC[:, h, blk * 128:(blk + 1) * 128], strips_all[:, 8 + h, dl + 1, :], [b_str], [b_sc9])
            P.flush()
        stage = [0]

        def stop_now():
            stage[0] += 1
            return CFG.get("STOP") is not None and stage[0] > CFG["STOP"]

        for l in range(DEPTH):
            if stop_now():
                break
            lam_init = 0.8 - 0.6 * math.exp(-0.3 * l)
            with TileAlloc(nc, P) as ta:
                gpre_t = ta.sb("gpre_t", [128, 16], F32)
                wld = [ta.sb("wld%d" % i, [128, 16, 128], F32) for i in range(2)]
                wcv = [ta.sb("wcv%d" % i, [128, 16, 128], BF16) for i in range(2)]
                rld = [ta.sb("rld%d" % i, [128, 2048], F32) for i in range(2)]
                rcv = [ta.sb("rcv%d" % i, [128, 2048], BF16) for i in range(2)]
                pwl = ta.sb("pwl", [128, 4, 128], F32)
                pel = ta.sb("pel", [128, 2, 2048], F32)
                lamt = ta.sb("lamt", [128, 4, 64], F32)
                lamp = ta.sb("lamp", [128, 2, 64], F32)
                lams = ta.sb("lams", [128, 2], F32)
                lame = ta.sb("lame", [128, 2], F32)
                sinkt = ta.sb("sinkt", [128, 8], F32)
                qnt = ta.sb("qnt", [128, 1], F32)
                sublt = ta.sb("sublt", [128, 1], F32)
                b_g = P.buf()
                P.dma("sp", gpre_t[:], gpreT[l, :, :], [], [b_g])
                b_wld = [P.buf(), P.buf()]
                b_wcv = [P.buf(), P.buf()]
                cnt = 0
                for c in range(44):
                    s = c % 2
                    P.dma("sp", wld[s][:], w_in[l, :, c * 128:(c + 1) * 128].rearrange("(kc p) m -> p kc m", p=128),
                          [], [b_wld[s]])
                    for kc in range(NKC):
                        if cnt % 2 == 0:
                            P.ts("dve", wcv[s][:, kc, :], wld[s][:, kc, :], gpre_t[:, kc:kc + 1], None, ALU.mult, None,
                                 [b_wld[s], b_g], [b_wcv[s]])
                        else:
                            P.act(wcv[s][:, kc, :], wld[s][:, kc, :], AF.Copy, [b_wld[s], b_g], [b_wcv[s]],
                                  scale=gpre_t[:, kc:kc + 1])
                        cnt += 1
                    if c < NFM:
                        P.dma("pool", wc[l, c, :, :], wcv[s][:].rearrange("p k m -> p (k m)"), [b_wcv[s]], [])
                    else:
                        g, hf = (c - NFM) // 2, (c - NFM) % 2
                        P.dma("pool", wtm[l, g, :, :, hf * 128:(hf + 1) * 128], wcv[s][:], [b_wcv[s]], [])
                b_rld = [P.buf(), P.buf()]
                b_rcv = [P.buf(), P.buf()]
                cnt = 0
                for src, dst in ((w_o, wo_s), (w_pg, wpg_s)):
                    for kc in range(NKC):
                        s = cnt % 2
                        P.dma("sp", rld[s][:], src[l, kc * 128:(kc + 1) * 128, :], [], [b_rld[s]])
                        P.cp("dve" if cnt % 2 == 0 else "act", rcv[s][:], rld[s][:], [b_rld[s]], [b_rcv[s]])
                        P.dma("pool", dst[l, kc, :, :], rcv[s][:], [b_rcv[s]], [])
                        cnt += 1
                b_v = P.buf()
                b_pl = P.buf()
                P.dma("sp", pel[:], w_pe[l, :, :].rearrange("(k p) n -> p k n", p=128), [], [b_pl])
                P.cp("dve", wpe_b[:], pel[:], [b_pl], [b_v])
                P.dma("sp", pwl[:], pool_w[l, :, :, :].rearrange("g c e -> c g e"), [], [b_pl])
                P.cp("dve", poolw_b[:], pwl[:], [b_pl], [b_v])
                P.dma("sp", gpost_bc[:], g_post[l:l + 1, :].partition_broadcast(128), [], [b_v])
                P.dma("sp", pscale_t[:], pscaleT[l, :, :], [], [b_v])
                P.dma("sp", gk_t[:], kn128[l, :, :], [], [b_v])
                P.dma("sp", qnt[:], qn128[l, :, :], [], [b_pl])
                P.ts("dve", gq_t[:], qnt[:], 0.125, None, ALU.mult, None, [b_pl], [b_v])
                P.dma("sp", sublt[:], subln[l, :, :], [], [b_pl])
                P.ts("dve", subg[:], sublt[:], float(1.0 - lam_init), None, ALU.mult, None, [b_pl], [b_v])
                P.dma("sp", sinkt[:], sink[l:l + 1, :].partition_broadcast(128), [], [b_pl])
                P.act(esink[:], sinkt[:], AF.Exp, [b_pl], [b_v])
                for h in range(8):
                    P.ts("dve", esink_bc[:, h, :], mmat[:, 0, :], 0.0, esink[:, h:h + 1], ALU.mult, ALU.add,
                         [b_v], [b_v])
                for i in range(4):
                    P.dma("sp", lamt[:, i, :], lam4[l, i:i + 1, :].partition_broadcast(128), [], [b_pl])
                b_l = P.buf()
                for i in range(2):
                    P.tt("dve", lamp[:, i, :], lamt[:, 2 * i, :], lamt[:, 2 * i + 1, :], ALU.mult, [b_pl], [b_l])
                    P.op("dve", lambda h, i=i: h.reduce_sum(out=lams[:, i:i + 1], in_=lamp[:, i, :], axis=AX.X),
                         [b_l], [b_l])
                P.act(lame[:], lams[:], AF.Exp, [b_l], [b_l])
                P.tt("dve", neglam[:], lame[:, 1:2], lame[:, 0:1], ALU.subtract, [b_l], [b_v])
                P.ts("dve", neglam[:], neglam[:], float(-lam_init), None, ALU.add, None, [b_v], [b_v])
                P.flush()

            for (sk, S) in seqs:
                xsrc = x_in[sk] if l == 0 else x1s[sk][(l - 1) % 2]
                xdst = y_out[sk] if l == DEPTH - 1 else x1s[sk][l % 2]
                if stop_now():
                    break
                view = (sk == "s" and l == DEPTH - 1)
                phase1(nc, P, l, sk, S, xsrc, locals(), view)
                if stop_now():
                    break
                phase2(nc, P, l, sk, S, xsrc, xdst, locals(), view)
    return nc


def phase1(nc, P, l, sk, S, xsrc, E, view=False):
    identb, bdb, rmat = E["identb"], E["bdb"], E["rmat"]
    gq_t, gk_t = E["gq_t"], E["gk_t"]
    wc, wtm = E["wc"], E["wtm"]
    uT, VA, VD, VC, BX = E["uT"][sk], E["VA"][sk], E["VD"][sk], E["VC"][sk], E["BX"][sk]
    cosT, sinT = (E["cosT2"], E["sinT2"]) if view else (E["cosT"], E["sinT"])
    NST = S // 1024
    KSET = (C_AK, C_CK, C_CK + 1, C_CK + 2, C_CK + 3, C_DK)
    with TileAlloc(nc, P) as ta:
        xin = [ta.sb("xin%d" % i, [128, 2048], F32) for i in range(2)]
        xsb = ta.sb("xsb", [128, 2048], BF16)
        ssq = [ta.sb("ssq%d" % i, [128, 1], F32) for i in range(2)]
        sdv = [ta.sb("sdv%d" % i, [128, 1], F32) for i in range(2)]
        rsv = [ta.sb("rsv%d" % i, [128, 1], F32) for i in range(2)]
        hT = ta.sb("hT", [128, 16, 1024], BF16)
        wsl = [ta.sb("wsl%d" % i, [128, 16, 128], BF16) for i in range(3)]
        wts = [ta.sb("wts%d" % i, [128, 16, 256], BF16) for i in range(2)]
        stg = [ta.sb("stg%d" % i, [128, 1024], BF16) for i in range(3)]
        vsAD = ta.sb("vsAD", [128, 4, 4, 3, 64], BF16)
        vsC = ta.sb("vsC", [128, 4, 4, 3, 64], BF16)
        bst = ta.sb("bst", [128, 4, 512], BF16)
        sq = ta.sb("sq", [128, 512], BF16)
        sdt = ta.sb("sdt", [128, 512], F32)
        rst = ta.sb("rst", [128, 512], F32)
        xn = ta.sb("xn", [128, 512], F32)
        t1 = ta.sb("t1", [128, 512], F32)
        cs = ta.sb("cs", [128, 1024], F32)
        sn = ta.sb("sn", [128, 1024], F32)
        ptp = [ta.ps("ptp%d" % i, [128, 1024], BF16) for i in range(2)]
        pf = [ta.ps("pf%d" % i, [128, 512], F32) for i in range(4)]
        pm = [ta.ps("pm%d" % i, [128, 512], F32) for i in range(2)]

        b_xin = [P.buf(), P.buf()]
        b_xsb = P.buf()
        b_st = [P.buf(), P.buf()]
        b_hT = [P.buf() for _ in range(8)]
        b_idx = P.buf()
        if view:
            idxt = ta.sb("idxt", [128, S // 128], mybir.dt.int32)
            P.dma("sp", idxt[:], E["rowidx_d"][:, :], [], [b_idx])
        b_wsl = [P.buf() for _ in range(3)]
        b_wts = [P.buf() for _ in range(2)]
        b_stg = [P.buf() for _ in range(3)]
        b_vs = P.buf()
        b_bst = P.buf()
        b_ptp = [P.buf(), P.buf()]
        b_pf = [P.buf() for _ in range(4)]
        b_pm = [P.buf(), P.buf()]
        b_d = P.buf()
        b_cs = P.buf()
        b_const = P.buf()

        P.memset("dve", vsAD[:, :, :, 1, :], 1.0, [b_vs])
        P.memset("dve", vsC[:, :, :, 2, :], 1.0, [b_vs])

        ev = 0
        for st in range(NST):
            t0 = st * 1024
            for blk in range(8):
                s = blk % 2
                r0 = t0 + blk * 128
                if view:
                    bi_ = r0 // 128
                    P.op("pool", lambda h, s=s, bi_=bi_: h.indirect_dma_start(
                        out=xin[s][:], out_offset=None, in_=xsrc[:, :],
                        in_offset=bass.IndirectOffsetOnAxis(ap=idxt[:, bi_:bi_ + 1], axis=0)),
                        [b_idx], [b_xin[s]], dma=True)
                else:
                    P.dma("sp", xin[s][:], xsrc[r0:r0 + 128, :], [], [b_xin[s]])
                P.act(xsb[:], xin[s][:], AF.Square, [b_xin[s]], [b_xsb, b_st[s]], accum_out=ssq[s][:])
                P.act(sdv[s][:], ssq[s][:], AF.Sqrt, [b_st[s]], [b_st[s]], scale=1.0 / D, bias=EPS)
                P.recip(rsv[s][:], sdv[s][:], [b_st[s]], [b_st[s]])
                P.ts("dve", xsb[:], xin[s][:], rsv[s][:, 0:1], None, ALU.mult, None, [b_xin[s], b_st[s]], [b_xsb])
                for g in range(2):
                    for i in range(8):
                        kc = g * 8 + i
                        P.tr(ptp[g][:, i * 128:(i + 1) * 128], xsb[:, kc * 128:(kc + 1) * 128], identb[:],
                             [b_xsb, b_const], [b_ptp[g]])
                    P.cp("dve" if g == 0 else "act", hT[:, g * 8:(g + 1) * 8, blk * 128:(blk + 1) * 128],
                         ptp[g][:].rearrange("p (k t) -> p k t", k=8), [b_ptp[g]], [b_hT[blk]])
            P.dma("sp", cs[:], cosT[:, t0:t0 + 1024], [], [b_cs])
            P.dma("sp", sn[:], sinT[:, t0:t0 + 1024], [], [b_cs])
            skip_nonk = view and (S // 4) % 1024 == 0 and st >= (S // 4096)
            pend1, pend2 = [], []

            def run_pending():
                for f in pend2[:]:
                    pend2.remove(f)
                    f()
                for f in pend1[:]:
                    pend1.remove(f)
                    f()

            for c in range(NFM):
                if skip_nonk and c not in KSET:
                    continue
                ws = c % 3
                sg = c % 3
                P.dma("sp", wsl[ws][:], wc[l, c, :, :].rearrange("p (k m) -> p k m", k=16), [], [b_wsl[ws]])
                for tt in range(2):
                    pi = (2 * c + tt) % 4
                    for kc in range(NKC):
                        P.mm(pf[pi][:], wsl[ws][:, kc, :], hT[:, kc, tt * 512:(tt + 1) * 512], kc == 0, kc == NKC - 1,
                             [b_wsl[ws]] + b_hT[tt * 4:tt * 4 + 4], [b_pf[pi]])
                    run_pending()

                    def evac(c=c, tt=tt, pi=pi, sg=sg):
                        nonlocal ev
                        dst = stg[sg][:, tt * 512:(tt + 1) * 512]

                        def store():
                            if tt == 1:
                                P.dma("pool", uT[c, :, t0:t0 + 1024], stg[sg][:], [b_stg[sg]], [])
                        if c < C_AK or C_CQ <= c < C_CK:
                            if ev % 2 == 0:
                                P.ts("dve", dst, pf[pi][:], 0.125, None, ALU.mult, None, [b_pf[pi]], [b_stg[sg]])
                            else:
                                P.act(dst, pf[pi][:], AF.Copy, [b_pf[pi]], [b_stg[sg]], scale=0.125)
                            ev += 1
                            store()
                        elif c == C_AK or C_CK <= c < C_CG:
                            P.cp("dve" if ev % 2 == 0 else "act", dst, pf[pi][:], [b_pf[pi]], [b_stg[sg]])
                            ev += 1
                            store()
                        elif c < C_DQ or c >= C_DG:
                            P.act(dst, pf[pi][:], AF.Silu, [b_pf[pi]], [b_stg[sg]])
                            store()
                        else:
                            gt = gq_t if c < C_DK else gk_t
                            P.act(sq[:], pf[pi][:], AF.Square, [b_pf[pi]], [b_d])
                            P.mm(pm[0][:], bdb[:], sq[:], True, True, [b_d, b_const], [b_pm[0]])
                            P.act(sdt[:], pm[0][:], AF.Sqrt, [b_pm[0]], [b_d], scale=1.0 / 64, bias=EPS)
                            P.recip(rst[:], sdt[:], [b_d], [b_d])
                            P.stt(xn[:], pf[pi][:], gt[:, 0:1], rst[:], ALU.mult, ALU.mult, [b_pf[pi], b_d], [b_d])

                            def stage2():
                                P.mm(pm[1][:], rmat[:], xn[:], True, True, [b_d, b_const], [b_pm[1]])
                                P.tt("dve", t1[:], xn[:], cs[:, tt * 512:(tt + 1) * 512], ALU.mult, [b_d, b_cs], [b_d])
                                P.tt("dve", sdt[:], pm[1][:], sn[:, tt * 512:(tt + 1) * 512], ALU.mult,
                                     [b_pm[1], b_cs, b_d], [b_d])
                                P.tt("dve", dst, t1[:], sdt[:], ALU.add, [b_d], [b_stg[sg]])
                                store()
                            pend2.append(stage2)
                    pend1.append(evac)
            run_pending()
            run_pending()
            for g in range(5):
                wsi = g % 2
                P.dma("sp", wts[wsi][:], wtm[l, g, :, :, :], [], [b_wts[wsi]])
                for blk in range(8):
                    pi = blk % 2
                    bq = blk % 4
                    for kc in range(NKC):
                        P.mm(pm[pi][:, 0:256], hT[:, kc, blk * 128:(blk + 1) * 128], wts[wsi][:, kc, :], kc == 0,
                             kc == NKC - 1, [b_wts[wsi], b_hT[blk]], [b_pm[pi]])
                    eng = "dve" if blk % 2 == 0 else "act"
                    if g == 0:
                        src = pm[pi][:, 0:256].rearrange("p (h d) -> p h d", h=4)
                        P.cp(eng, vsAD[:, bq, :, 0, :], src, [b_pm[pi]], [b_vs])
                        P.cp("dve" if eng == "act" else "act", vsAD[:, bq, :, 2, :], src, [b_pm[pi]], [b_vs])
                    elif g in (1, 2):
                        h0 = (g - 1) * 2
                        src = pm[pi][:, 0:256].rearrange("p (h a d) -> p h a d", h=2, a=2)
                        P.cp(eng, vsC[:, h0:h0 + 2, bq, 0:2, :], src, [b_pm[pi]], [b_vs])
                    else:
                        c0 = (g - 3) * 256
                        P.cp(eng, bst[:, bq, c0:c0 + 256], pm[pi][:, 0:256], [b_pm[pi]], [b_bst])
                    if bq == 3:
                        k0 = (blk // 4) * 4
                        if g == 0:
                            P.dma("pool", VA[st, :, k0:k0 + 4, :],
                                  vsAD[:, :, 0:2, :, :].rearrange("p b h a d -> p b (h a d)"), [b_vs], [])
                            P.dma("pool", VD[st, :, k0:k0 + 4, :],
                                  vsAD[:, :, 2:4, :, :].rearrange("p b h a d -> p b (h a d)"), [b_vs], [])
                        elif g in (1, 2):
                            h0 = (g - 1) * 2
                            for hh in range(2):
                                P.dma("pool", VC[h0 + hh, st, :, k0:k0 + 4, :],
                                      vsC[:, h0 + hh, :, :, :].rearrange("p b a d -> p b (a d)"), [b_vs], [])
                        else:
                            rb = t0 + k0 * 128
                            c0 = (g - 3) * 256
                            P.dma("pool", BX[rb:rb + 512, c0:c0 + 256].rearrange("(b p) c -> p b c", p=128),
                                  bst[:, :, c0:c0 + 256], [b_bst], [])
        P.flush()


def phase2(nc, P, l, sk, S, xsrc, xdst, E, view=False):
    identb, onesb, mmat = E["identb"], E["onesb"], E["mmat"]
    strips, farb, wpe_b, gpost_bc, poolw_b = E["strips"], E["farb"], E["wpe_b"], E["gpost_bc"], E["poolw_b"]
    stripsC, esink_bc = E["stripsC"], E["esink_bc"]
    pscale_t, esink, neglam, subg = E["pscale_t"], E["esink"], E["neglam"], E["subg"]
    wo_s, wpg_s = E["wo_s"], E["wpg_s"]
    uT, VA, VD, VC, BX = E["uT"][sk], E["VA"][sk], E["VD"][sk], E["VC"][sk], E["BX"][sk]
    peT = E["peT_in"][sk]
    NQT = S // 512
    NKB = S // 128
    NG = S // 1024
    NQR = NQT // 4 if view else NQT
    with TileAlloc(nc, P) as ta:
        qt_t = [ta.sb("qt%d" % i, [128, 4, 512], BF16) for i in range(2)]
        gt_t = [ta.sb("gt%d" % i, [128, 4, 512], BF16) for i in range(2)]
        kt_t = [ta.sb("kt%d" % i, [128, 1024], BF16) for i in range(3)]
        v_t = [ta.sb("v%d" % i, [128, 8, 384], BF16) for i in range(3)]
        pt_t = [ta.sb("pt%d" % i, [128, 1024], BF16) for i in range(3)]
        bigT = ta.sb("bigT", [128, 16, 512], BF16)
        bxs = ta.sb("bxs", [128, 6, 512], BF16)
        pooled = [ta.sb("pooled%d" % i, [128, 512], BF16) for i in range(1)]
        ysq = ta.sb("ysq", [128, 512], BF16)
        NWPS, NWOS = (3, 2) if view else (4, 3)
        wps = [ta.sb("wps%d" % i, [128, 512], BF16) for i in range(NWPS)]
        wos = [ta.sb("wos%d" % i, [128, 1024], BF16) for i in range(NWOS)]
        peb = ta.sb("peb", [128, 2, 512], BF16)
        pef = ta.sb("pef", [128, 2, 512], F32)
        osb = [ta.sb("osb%d" % i, [128, 2048], F32) for i in range(4)]
        xres = ta.sb("xres", [128, 2048], F32)
        x1b = ta.sb("x1b", [128, 2048], BF16)
        sc = [ta.sb("sc%d" % i, [128, 512], F32) for i in range(3)]
        tmp = [ta.sb("tmp%d" % i, [128, 1024], F32) for i in range(2)]
        tmpf = ta.sb("tmpf", [128, 512], F32)
        ssq = ta.sb("ssq", [128, 16], F32)
        ssr = ta.sb("ssr", [128, 4], F32)
        sdr = ta.sb("sdr", [128, 4], F32)
        rsr = ta.sb("rsr", [128, 4], F32)
        st_ps = [ta.ps("st%d" % i, [128, 1024], F32) for i in range(2)]
        ac_ps = [ta.ps("ac%d" % i, [128, 1024], F32) for i in range(2)]

        bank = [st_ps[0][:, 0:512], st_ps[0][:, 512:1024], st_ps[1][:, 0:512], st_ps[1][:, 512:1024],
                ac_ps[0][:, 0:512], ac_ps[0][:, 512:1024], ac_ps[1][:, 0:512], ac_ps[1][:, 512:1024]]
        b_bank = [P.buf() for _ in range(8)]
        b_qt = [P.buf(), P.buf()]
        b_gt = [P.buf(), P.buf()]
        b_kt = [P.buf() for _ in range(3)]
        b_v = [P.buf() for _ in range(3)]
        b_pt = [P.buf() for _ in range(3)]
        b_big = [P.buf() for _ in range(16)]
        b_bxs = P.buf()
        b_pooled = [P.buf()]
        b_ysq = P.buf()
        b_junk = P.buf()
        b_wps = [P.buf() for _ in range(NWPS)]
        b_wos = [P.buf() for _ in range(NWOS)]
        b_pe = P.buf()
        b_osb = [P.buf() for _ in range(4)]
        b_xres = P.buf()
        b_x1b = P.buf()
        b_sc = [P.buf() for _ in range(3)]
        b_tmp = [P.buf(), P.buf()]
        b_tmp2 = [[P.buf(), P.buf()], [P.buf(), P.buf()]]
        b_tmpf = P.buf()
        b_ss = P.buf()
        b_const = P.buf()
        rr = {"qt": 0, "gt": 0, "kt": 0, "v": 0, "pt": 0, "st": 0, "sc": 0, "tmp": 0, "wps": 0, "wos": 0, "acc": 0}
        b_view = P.buf()
        if view:
            idxt = ta.sb("idxt2", [128, S // 128], mybir.dt.int32)
            farbC = ta.sb("farbC", [128, NKB, 4], F32)
            isrt = ta.sb("isrt", [128, NKB], F32)
            flg = ta.sb("flg", [128, 4], F32)
            omf = ta.sb("omf", [128, 4], F32)
            drl = ta.sb("drl", [128, 4], F32)
            cseam = ta.sb("cseam", [128, 8], F32)
            mmv = ta.sb("mmv", [128, 16, 128], BF16)
            b_vl = P.buf()
            P.dma("sp", idxt[:], E["rowidx_d"][:, :], [], [b_view])
            P.dma("sp", isrt[:], E["isr_d"][0:1, :].partition_broadcast(128), [], [b_vl])
            P.dma("sp", flg[:], E["flags_d"][0:1, :].partition_broadcast(128), [], [b_vl])
            P.dma("sp", mmv[:], E["mmatv_d"][:, :, :], [], [b_view])
            P.tt("dve", drl[:], farb[:, 1, 8:12], farb[:, 0, 8:12], ALU.subtract, [b_const], [b_vl])
            for h in range(4):
                P.ts("dve", farbC[:, :, h], isrt[:], drl[:, h:h + 1], farb[:, 0, 8 + h:9 + h], ALU.mult, ALU.add,
                     [b_vl, b_const], [b_view])
            P.ts("dve", omf[:], flg[:], -1.0, 1.0, ALU.mult, ALU.add, [b_vl], [b_vl])
            P.ts("dve", cseam[:, 0:4], farb[:, 1, 8:12], omf[:, 0:1], None, ALU.mult, None, [b_vl, b_const], [b_view])
            P.ts("dve", cseam[:, 4:8], farb[:, 0, 8:12], omf[:, 1:2], None, ALU.mult, None, [b_vl, b_const], [b_view])

        def nxt(k, n):
            v = rr[k]
            rr[k] = (v + 1) % n
            return v

        preloaded = {}

        def load_group(kt_chunk, Vsrc, vkey, W, kbs, g):
            k0 = max(g * 8, min(kbs))
            k1 = min(g * 8 + 8, max(kbs) + 1)
            key = (kt_chunk, vkey, g, k0, k1)
            if key in preloaded:
                return preloaded.pop(key)
            ks = nxt("kt", 3)
            vs = nxt("v", 3)
            P.dma("sp", kt_t[ks][:, (k0 - g * 8) * 128:(k1 - g * 8) * 128],
                  uT[kt_chunk, :, k0 * 128:k1 * 128], [], [b_kt[ks]])
            P.dma("sp", v_t[vs][:, k0 - g * 8:k1 - g * 8, 0:W], Vsrc(g)[:, k0 - g * 8:k1 - g * 8, :], [], [b_v[vs]])
            return ks, vs

        def attention(kind, qsl, qi, pidx, kbs, kt_chunk, Vsrc, q_blk0, head_ids, evac, vkey=None, nxt_spec=None):
            if kind == "C":
                accs = [4, 5, 6, 7]
            else:
                a0 = 4 + 2 * nxt("acc", 2)
                accs = [a0, a0 + 1]
            groups = sorted(set(kb // 8 for kb in kbs))
            ktslot, vslot = {}, {}
            W = 192 if kind == "C" else 384

            def ensure(gi):
                if gi >= len(groups) or groups[gi] in ktslot:
                    return
                g = groups[gi]
                ktslot[g], vslot[g] = load_group(kt_chunk, Vsrc, vkey, W, kbs, g)

            ensure(0)
            ensure(1)
            nk = len(kbs)
            pend = []

            def issue_qk(i):
                kb = kbs[i]
                g = kb // 8
                ko = (kb - g * 8) * 128
                ss = nxt("st", 2)
                for e in range(2):
                    P.mm(bank[2 * ss + e], kt_t[ktslot[g]][e * 64:(e + 1) * 64, ko:ko + 128],
                         qt_t[qsl][e * 64:(e + 1) * 64, qi, :], True, True,
                         [b_kt[ktslot[g]], b_qt[qsl]], [b_bank[2 * ss + e]], tp=(e * 64, 0))
                return ss

            def issue_exp(i, ss):
                kb = kbs[i]
                j = kb - q_blk0
                if view:
                    j = ((j + NKB // 2) % NKB) - NKB // 2
                ps_ = nxt("pt", 3)
                src_b = [b_bank[2 * ss], b_bank[2 * ss + 1]]
                if kind == "D":
                    P.act(pt_t[ps_][:], st_ps[ss][:], AF.Exp, src_b, [b_pt[ps_]])
                elif kind == "C" and (j < -1 or j > 4):
                    side = 0 if j < -1 else 1
                    h = head_ids[0]
                    if view and kb >= q_blk0:
                        P.act(pt_t[ps_][:], st_ps[ss][:], AF.Exp, src_b + [b_view], [b_pt[ps_]],
                              bias=farbC[:, kb, h:h + 1])
                    else:
                        P.act(pt_t[ps_][:], st_ps[ss][:], AF.Exp, src_b + [b_const], [b_pt[ps_]],
                              bias=farb[:, side, 8 + h:9 + h])
                else:
                    tm = nxt("tmp", 2)
                    h = head_ids[0]
                    s0 = (4 - j) * 128
                    in1 = stripsC[:, h, s0:s0 + 512]
                    xr = [b_const]
                    if view and ((qt == 0 and j == -1) or (qt == NQR - 1 and j == 4)):
                        lohi = 0 if j == -1 else 1
                        P.ts("dve", tmpf[:], stripsC[:, h, s0:s0 + 512], flg[:, lohi:lohi + 1],
                             cseam[:, 4 * lohi + h:4 * lohi + h + 1], ALU.mult, ALU.add, [b_const, b_view], [b_tmpf])
                        in1 = tmpf[:]
                        xr = [b_tmpf]
                    for e in range(2):
                        P.tt("dve", tmp[tm][:, e * 512:(e + 1) * 512], st_ps[ss][:, e * 512:(e + 1) * 512],
                             in1, ALU.add, [b_bank[2 * ss + e]] + xr, [b_tmp2[tm][e]])
                    P.act(pt_t[ps_][:], tmp[tm][:], AF.Exp, b_tmp2[tm], [b_pt[ps_]])
                return ps_

            def issue_pv(i, ps_):
                kb = kbs[i]
                g = kb // 8
                kbi = kb - g * 8
                vt = v_t[vslot[g]]
                first, last = (i == 0), (i == nk - 1)
                if i == 0 or kbs[i - 1] // 8 != g:
                    ensure(groups.index(g) + 2)
                    if nxt_spec is not None and groups.index(g) == len(groups) - 1:
                        n_chunk, n_vsrc, n_vkey, n_w, n_kbs = nxt_spec
                        n_groups = sorted(set(kb_ // 8 for kb_ in n_kbs))
                        for g2 in n_groups[:2]:
                            k0 = max(g2 * 8, min(n_kbs))
                            k1 = min(g2 * 8 + 8, max(n_kbs) + 1)
                            key = (n_chunk, n_vkey, g2, k0, k1)
                            if key not in preloaded:
                                preloaded[key] = load_group(n_chunk, n_vsrc, n_vkey, n_w, n_kbs, g2)
                for e in range(2):
                    rhs = pt_t[ps_][:, e * 512:(e + 1) * 512]
                    if kind == "C":
                        P.mm(bank[accs[2 * e]], vt[:, kbi, 0:128], rhs, first, last,
                             [b_v[vslot[g]], b_pt[ps_]], [b_bank[accs[2 * e]]])
                        P.mm(bank[accs[2 * e + 1]], onesb[:], rhs, first, last,
                             [b_const, b_pt[ps_]], [b_bank[accs[2 * e + 1]]])
                    else:
                        vo = e * 192 + (64 if (pidx % 2) else 0)
                        P.mm(bank[accs[e]], vt[:, kbi, vo:vo + 128], rhs, first, last,
                             [b_v[vslot[g]], b_pt[ps_]], [b_bank[accs[e]]])

            ssq_ = [issue_qk(0)]
            if nk > 1:
                ssq_.append(issue_qk(1))
            for i in range(nk):
                ps_ = issue_exp(i, ssq_[i])
                if CFG.get("LOOK2", True):
                    if i + 2 < nk:
                        ssq_.append(issue_qk(i + 2))
                    issue_pv(i, ps_)
                else:
                    issue_pv(i, ps_)
                    if i + 2 < nk:
                        ssq_.append(issue_qk(i + 2))
            evac(accs)

        for qt in range(NQR):
            q0 = qt * 512
            qb0 = qt * 4
            qsl = nxt("qt", 2)
            gsl = nxt("gt", 2)
            P.dma("sp", qt_t[qsl][:], uT[C_AQ:C_AQ + 4, :, q0:q0 + 512].rearrange("c p t -> p c t"), [], [b_qt[qsl]])
            P.dma("sp", gt_t[gsl][:], uT[C_AG:C_AG + 4, :, q0:q0 + 512].rearrange("c p t -> p c t"), [], [b_gt[gsl]])
            if view:
                kbsA = [(qb0 - 1) % NKB] + list(range(qb0, qb0 + 5))
            else:
                kbsA = [kb for kb in range(qb0 - 1, qb0 + 5) if 0 <= kb < NKB]
            ksA = nxt("kt", 3)
            vsA = nxt("v", 3)
            runs = []
            for ii, kb in enumerate(kbsA):
                if runs and kbsA[ii - 1] + 1 == kb and kb // 8 == kbsA[ii - 1] // 8:
                    runs[-1][2] += 1
                else:
                    runs.append([ii, kb, 1])
            for ii, kb, n in runs:
                g = kb // 8
                P.dma("sp", kt_t[ksA][:, ii * 128:(ii + n) * 128], uT[C_AK, :, kb * 128:(kb + n) * 128], [], [b_kt[ksA]])
                P.dma("sp", v_t[vsA][:, ii:ii + n, :], VA[g, :, kb - g * 8:kb - g * 8 + n, :], [], [b_v[vsA]])
            for ob in range(4):
                qb = qb0 + ob
                dls = [dl for dl in (-1, 0, 1) if view or 0 <= qb + dl < NKB]
                pts = []
                for dl in dls:
                    kl = kbsA.index((qb + dl) % NKB)
                    ss = nxt("st", 2)
                    for jj in range(4):
                        for e in range(2):
                            hh = jj + 4 * e
                            P.mm(st_ps[ss][:, hh * 128:(hh + 1) * 128],
                                 kt_t[ksA][e * 64:(e + 1) * 64, kl * 128:(kl + 1) * 128],
                                 qt_t[qsl][e * 64:(e + 1) * 64, jj, ob * 128:(ob + 1) * 128], True, True,
                                 [b_kt[ksA], b_qt[qsl]], [b_bank[2 * ss + e]], tp=(e * 64, 0))
                    tm = nxt("tmp", 2)
                    P.tt("dve", tmp[tm][:].rearrange("p (h q) -> p h q", h=8),
                         st_ps[ss][:].rearrange("p (h q) -> p h q", h=8), strips[:, 0:8, dl + 1, :], ALU.add,
                         [b_bank[2 * ss], b_bank[2 * ss + 1], b_const], b_tmp2[tm])
                    ps_ = nxt("pt", 3)
                    if view and qt == 0 and ob == 0 and dl == -1:
                        P.act(pt_t[ps_][:], tmp[tm][:], AF.Exp, b_tmp2[tm] + [b_view], [b_pt[ps_]], bias=flg[:, 2:3])
                    elif view and qt == NQR - 1 and ob == 3 and dl == 1:
                        P.act(pt_t[ps_][:], tmp[tm][:], AF.Exp, b_tmp2[tm] + [b_view], [b_pt[ps_]], bias=flg[:, 3:4])
                    else:
                        P.act(pt_t[ps_][:], tmp[tm][:], AF.Exp, b_tmp2[tm], [b_pt[ps_]])
                    pts.append((ps_, kl))
                ai = nxt("acc", 2)
                acc = ac_ps[ai]
                for hh in range(8):
                    jj, e = hh % 4, hh // 4
                    vo = e * 192 + (64 if (jj % 2) else 0)
                    for ii, (ps_, kl) in enumerate(pts):
                        P.mm(acc[:, hh * 128:(hh + 1) * 128], v_t[vsA][:, kl, vo:vo + 128],
                             pt_t[ps_][:, hh * 128:(hh + 1) * 128], ii == 0, ii == len(pts) - 1,
                             [b_v[vsA], b_pt[ps_]], [b_bank[4 + 2 * ai + e]])
                accb = [b_bank[4 + 2 * ai], b_bank[5 + 2 * ai]]
                cols = slice(ob * 128, (ob + 1) * 128)
                for par in range(2):
                    orow = slice(64, 128) if par else slice(0, 64)
                    srow = slice(0, 64) if par else slice(64, 128)
                    vw = lambda ap: ap.rearrange("p (e j r q) -> p e j r q", e=2, j=2, r=2)[:, :, :, par, :]
                    si = nxt("sc", 3)
                    scv = sc[si][orow, :].rearrange("p (e j q) -> p e j q", e=2, j=2)
                    P.cp("dve", scv, vw(acc[srow, :]), accb, [b_sc[si]])
                    P.tt("dve", scv, scv, vw(esink_bc[orow, :, :].rearrange("p h q -> p (h q)")), ALU.add,
                         [b_sc[si], b_const], [b_sc[si]])
                    P.recip(sc[si][orow, :], sc[si][orow, :], [b_sc[si]], [b_sc[si]])
                    P.tt("dve", scv, vw(acc[orow, :]), scv, ALU.mult, accb + [b_sc[si]], [b_sc[si]])
                    P.tt("dve", bigT[orow, 0:4, cols].rearrange("p (e j) q -> p e j q", e=2), scv,
                         gt_t[gsl][orow, 0:4, cols].rearrange("p (e j) q -> p e j q", e=2), ALU.mult,
                         [b_sc[si], b_gt[gsl]], b_big[0:4])

            if CFG.get("P2STOP") == 1:
                continue
            gslB = nxt("gt", 2)
            P.dma("sp", gt_t[gslB][:], uT[C_BG:C_BG + 4, :, q0:q0 + 512].rearrange("c p t -> p c t"), [], [b_gt[gslB]])
            lo = max(qb0 - 1, 0)
            hi = min(qb0 + 5, NKB)
            P.dma("sp", bxs[:, lo - (qb0 - 1):hi - (qb0 - 1), :],
                  BX[lo * 128:hi * 128, :].rearrange("(b p) c -> p b c", p=128), [], [b_bxs])
            if view and qb0 == 0:
                P.dma("sp", bxs[:, 0, :], BX[(NKB - 1) * 128:NKB * 128, :], [], [b_bxs])
            for g in range(4):
                pb = 4 + (g % 2) * 2
                for ob in range(4):
                    qb = qb0 + ob
                    dls = [dl for dl in (-1, 0, 1) if view or 0 <= qb + dl < NKB]
                    for ii, dl in enumerate(dls):
                        if dl == -1:
                            mi = 0
                        elif dl == 1:
                            mi = 2
                        else:
                            mi = 3 if qb == 0 else (4 if qb == NKB - 1 else 1)
                        mt = mmat[:, g * 5 + mi, :]
                        xr = [b_const]
                        if view and qb == 0 and dl in (-1, 0):
                            mt = mmv[:, g * 4 + (0 if dl == -1 else 1), :]
                            xr = [b_view]
                        elif view and qb == 4 * NQR - 1 and dl in (0, 1):
                            mt = mmv[:, g * 4 + (2 if dl == 0 else 3), :]
                            xr = [b_view]
                        elif view and dl == 0:
                            mt = mmat[:, g * 5 + 1, :]
                        P.mm(bank[pb][:, ob * 128:(ob + 1) * 128], bxs[:, ob + dl + 1, g * 128:(g + 1) * 128],
                             mt, ii == 0, ii == len(dls) - 1, [b_bxs] + xr, [b_bank[pb]])
                pl = 0
                P.cp("act", pooled[pl][:], bank[pb], [b_bank[pb]], [b_pooled[pl]])
                P.mm(bank[pb + 1], poolw_b[:, g, :], pooled[pl][:], True, True, [b_pooled[pl], b_const], [b_bank[pb + 1]])
                P.stt(bigT[:, 4 + g, :], bank[pb + 1], pscale_t[:, g:g + 1], gt_t[gslB][:, g, :], ALU.mult, ALU.mult,
                      [b_bank[pb + 1], b_gt[gslB], b_const], [b_big[4 + g]])

            if CFG.get("P2STOP") == 2:
                continue
            def evac_AD(kind, j, gsl_):
                def f(accs):
                    for e in range(2):
                        h = j + 4 * e
                        mchunk = (h // 2) if kind == "A" else 12 + (h // 2)
                        gidx = h // 2
                        odd = j % 2
                        orow = slice(64, 128) if odd else slice(0, 64)
                        srow = slice(0, 64) if odd else slice(64, 128)
                        a = accs[e]
                        si = nxt("sc", 3)
                        if kind == "A":
                            P.cp("dve", sc[si][orow, :], bank[a][srow, :], [b_bank[a]], [b_sc[si]])
                            P.ts("dve", sc[si][orow, :], sc[si][orow, :], esink[orow, h:h + 1], None, ALU.add, None,
                                 [b_sc[si], b_const], [b_sc[si]])
                            P.recip(sc[si][orow, :], sc[si][orow, :], [b_sc[si]], [b_sc[si]])
                        else:
                            P.recip(sc[si][orow, :], bank[a][srow, :], [b_bank[a]], [b_sc[si]])
                        P.tt("dve", sc[si][orow, :], bank[a][orow, :], sc[si][orow, :], ALU.mult,
                             [b_bank[a], b_sc[si]], [b_sc[si]])
                        P.tt("dve", bigT[orow, mchunk, :], sc[si][orow, :], gt_t[gsl_][orow, gidx, :], ALU.mult,
                             [b_sc[si], b_gt[gsl_]], [b_big[mchunk]])
                return f

            qsl = nxt("qt", 2)
            gsl = nxt("gt", 2)
            P.dma("sp", qt_t[qsl][:], uT[C_DQ:C_DQ + 4, :, q0:q0 + 512].rearrange("c p t -> p c t"), [], [b_qt[qsl]])
            P.dma("sp", gt_t[gsl][:], uT[C_DG:C_DG + 4, :, q0:q0 + 512].rearrange("c p t -> p c t"), [], [b_gt[gsl]])
            allkb = list(range(NKB))
            vsrcD = lambda g: VD[g, :, :, :]
            vsrcC = [(lambda g, h=h: VC[h, g, :, :, :]) for h in range(4)]
            for j in range(4):
                nspec = (C_DK, vsrcD, "D", 384, allkb) if j < 3 else (C_CK, vsrcC[0], ("C", 0), 192, allkb)
                attention("D", qsl, j, j, allkb, C_DK, vsrcD, qb0, (j, j + 4),
                          evac_AD("D", j, gsl), vkey="D", nxt_spec=nspec)

            if CFG.get("P2STOP") == 3:
                continue
            qsl = nxt("qt", 2)
            gsl = nxt("gt", 2)
            P.dma("sp", qt_t[qsl][:], uT[C_CQ:C_CQ + 4, :, q0:q0 + 512].rearrange("c p t -> p c t"), [], [b_qt[qsl]])
            P.dma("sp", gt_t[gsl][:], uT[C_CG:C_CG + 4, :, q0:q0 + 512].rearrange("c p t -> p c t"), [], [b_gt[gsl]])

            def evac_C(h, gsl_):
                def f(accs):
                    R, T = tmp[0], tmp[1]
                    for e in range(2):
                        ao, asum = accs[2 * e], accs[2 * e + 1]
                        P.recip(R[:, e * 512:(e + 1) * 512], bank[asum], [b_bank[asum]], [b_tmp2[0][e]])
                        P.tt("dve", T[:, e * 512:(e + 1) * 512], bank[ao], R[:, e * 512:(e + 1) * 512],
                             ALU.mult, [b_bank[ao], b_tmp2[0][e]], [b_tmp2[1][e]])
                    P.stt(T[:, 0:512], T[:, 512:1024], neglam[:, 0:1], T[:, 0:512], ALU.mult, ALU.add,
                          [b_tmp2[1][0], b_tmp2[1][1], b_const], [b_tmp2[1][0], b_tmp2[1][1]])
                    P.act(ysq[:], T[:, 0:512], AF.Square, [b_tmp2[1][0], b_tmp2[1][1]], [b_ysq])
                    sb_ = accs[0]
                    P.mm(bank[sb_], onesb[:], ysq[:], True, True, [b_ysq, b_const], [b_bank[sb_]])
                    s1 = nxt("sc", 3)
                    P.act(sc[s1][:], bank[sb_], AF.Sqrt, [b_bank[sb_]], [b_sc[s1]], scale=1.0 / 128, bias=EPS)
                    P.recip(sc[s1][:], sc[s1][:], [b_sc[s1]], [b_sc[s1]])
                    P.stt(T[:, 0:512], T[:, 0:512], subg[:, 0:1], sc[s1][:], ALU.mult, ALU.mult,
                          [b_tmp2[1][0], b_tmp2[1][1], b_sc[s1], b_const], [b_tmp2[1][0], b_tmp2[1][1]])
                    P.tt("dve", bigT[:, 8 + h, :], T[:, 0:512], gt_t[gsl_][:, h, :], ALU.mult,
                         [b_tmp2[1][0], b_tmp2[1][1], b_gt[gsl_]], [b_big[8 + h]])
                return f

            for h in range(4):
                nspec = (C_CK + h + 1, vsrcC[h + 1], ("C", h + 1), 192, allkb) if h < 3 else None
                attention("C", qsl, h, 0, allkb, C_CK + h, vsrcC[h], qb0, (h, h),
                          evac_C(h, gsl), vkey=("C", h), nxt_spec=nspec)

            if CFG.get("P2STOP") == 4:
                continue
            if view:
                P.dma("sp", pef[:], E["peT_s2"][:, q0:q0 + 512].rearrange("(k p) t -> p k t", p=128), [], [b_pe])
            else:
                P.dma("sp", pef[:], peT[l, :, q0:q0 + 512].rearrange("(k p) t -> p k t", p=128), [], [b_pe])
            P.cp("act", peb[:], pef[:], [b_pe], [b_pe])
            for ch in range(2):
                for kc in range(NKC):
                    ws = nxt("wos", NWOS)
                    P.dma("sp", wos[ws][:], wo_s[l, kc, :, ch * 1024:(ch + 1) * 1024], [], [b_wos[ws]])
                    for tb in range(4):
                        for cb in range(2):
                            bi = tb * 2 + cb
                            P.mm(bank[bi], bigT[:, kc, tb * 128:(tb + 1) * 128], wos[ws][:, cb * 512:(cb + 1) * 512],
                                 kc == 0, kc == NKC - 1, [b_big[kc], b_wos[ws]], [b_bank[bi]])
                for tb in range(4):
                    for cb in range(2):
                        bi = tb * 2 + cb
                        c0 = ch * 1024 + cb * 512
                        P.cp("dve" if cb == 0 else "act", osb[tb][:, c0:c0 + 512], bank[bi], [b_bank[bi]], [b_osb[tb]])
            for tb in range(4):
                P.act(x1b[:], osb[tb][:], AF.Square, [b_osb[tb]], [b_x1b, b_ss], accum_out=ssr[:, tb:tb + 1])
            P.act(sdr[:], ssr[:], AF.Sqrt, [b_ss], [b_ss], scale=1.0 / D, bias=EPS)
            P.recip(rsr[:], sdr[:], [b_ss], [b_ss])
            if CFG.get("P2STOP") == 42:
                continue
            for tb in range(4):
                r0 = q0 + tb * 128
                if view:
                    bi_ = r0 // 128
                    P.op("pool", lambda h, bi_=bi_: h.indirect_dma_start(
                        out=xres[:], out_offset=None, in_=xsrc[:, :],
                        in_offset=bass.IndirectOffsetOnAxis(ap=idxt[:, bi_:bi_ + 1], axis=0)),
                        [b_view], [b_xres], dma=True)
                else:
                    P.dma("sp", xres[:], xsrc[r0:r0 + 128, :], [], [b_xres])
                P.stt(osb[tb][:], osb[tb][:], rsr[:, tb:tb + 1], gpost_bc[:], ALU.mult, ALU.mult,
                      [b_osb[tb], b_ss, b_const], [b_osb[tb]])
                P.tt("dve", osb[tb][:], osb[tb][:], xres[:], ALU.add, [b_osb[tb], b_xres], [b_osb[tb]])
                if CFG.get("P2STOP") == 43:
                    continue
                P.cp("act", x1b[:], osb[tb][:], [b_osb[tb]], [b_x1b])
                for g4 in range(4):
                    bi = (tb * 4 + g4) % 8
                    for i in range(4):
                        kc = g4 * 4 + i
                        P.mm(bank[bi][:, i * 128:(i + 1) * 128], x1b[:, kc * 128:(kc + 1) * 128], identb[:], True, True,
                             [b_x1b, b_const], [b_bank[bi]])
                    P.cp("act" if g4 % 2 == 0 else "dve", bigT[:, g4 * 4:(g4 + 1) * 4, tb * 128:(tb + 1) * 128],
                         bank[bi].rearrange("p (k t) -> p k t", k=4), [b_bank[bi]], b_big[g4 * 4:(g4 + 1) * 4])
            if CFG.get("P2STOP") == 5:
                continue
            for cq in range(4):
                c0 = cq * 512
                for kc in range(NKC):
                    ws = nxt("wps", NWPS)
                    P.dma("sp", wps[ws][:], wpg_s[l, kc, :, c0:c0 + 512], [], [b_wps[ws]])
                    for tb in range(4):
                        P.mm(bank[tb], bigT[:, kc, tb * 128:(tb + 1) * 128], wps[ws][:], kc == 0, kc == NKC - 1,
                             [b_big[kc], b_wps[ws]], [b_bank[tb]])
                for tb in range(4):
                    for k2 in range(2):
                        P.mm(bank[4 + tb], peb[:, k2, tb * 128:(tb + 1) * 128], wpe_b[:, k2, c0:c0 + 512], k2 == 0, k2 == 1,
                             [b_pe, b_const], [b_bank[4 + tb]])
                for tb in range(4):
                    si = nxt("sc", 3)
                    P.act(sc[si][:], bank[tb], AF.Sigmoid, [b_bank[tb]], [b_sc[si]])
                    P.tt("dve", tmpf[:], bank[4 + tb], sc[si][:], ALU.mult, [b_bank[4 + tb], b_sc[si]], [b_tmpf])
                    P.tt("dve", osb[tb][:, c0:c0 + 512], osb[tb][:, c0:c0 + 512], tmpf[:], ALU.add,
                         [b_osb[tb], b_tmpf], [b_osb[tb]])
            for tb in range(4):
                r0 = q0 + tb * 128
                P.dma("pool", xdst[r0:r0 + 128, :], osb[tb][:], [b_osb[tb]], [])
        P.flush()


_PROG_CACHE = {}


def _prep_inputs(inp, SP, SS, DEPTH, ncores):
    cols = dev_cols()
    f = lambda a: np.ascontiguousarray(np.asarray(a, dtype=np.float32))
    shared = {
        "w_in": f(np.asarray(inp["w_in"])[:, :, cols]),
        "w_o": f(inp["w_o"]),
        "w_pg": f(inp["w_pg"]),
        "w_pe": f(inp["w_pe"]),
        "gpreT": f(np.asarray(inp["g_pre"]).reshape(DEPTH, 16, 128).transpose(0, 2, 1)),
        "g_post": f(inp["g_post"]),
        "sink": f(inp["sink_a"]),
        "pool_w": f(inp["pool_w"]),
        "pscaleT": f(np.asarray(inp["pool_scale"]).reshape(DEPTH, 4, 128).transpose(0, 2, 1)),
        "lam4": f(np.stack([np.asarray(inp[k]) for k in ("lam_q1", "lam_k1", "lam_q2", "lam_k2")], axis=1)),
        "subln": f(np.asarray(inp["diff_subln"]).reshape(DEPTH, 128, 1)),
        "qn128": f(np.tile(np.asarray(inp["qnorm_d"]), (1, 2)).reshape(DEPTH, 128, 1)),
        "kn128": f(np.tile(np.asarray(inp["knorm_d"]), (1, 2)).reshape(DEPTH, 128, 1)),
        "relb": f(inp["rel_bias"]),
    }
    shared.update(host_consts(max(SP, SS)))
    xp = np.asarray(inp["x_prompt"])
    xs = np.asarray(inp["x_sample"])
    pp = np.asarray(inp["p_prompt"])
    psm = np.asarray(inp["p_sample"])
    nb_p, nb_s = xp.shape[0], xs.shape[0]
    nkb = SS // 128
    q4 = SS // 4
    mm = np.asarray(shared["mmat"]).astype(np.float32)
    maps = []
    for i in range(ncores):
        m = dict(shared)
        b = i % nb_s
        r = (i // nb_s) % 4
        m["x_p"] = f(xp[i % nb_p])
        m["x_s"] = f(xs[b])
        m["peT_p"] = f(pp[:, i % nb_p].transpose(0, 2, 1))
        m["peT_s"] = f(psm[:, b].transpose(0, 2, 1))
        vtok = (np.arange(SS) + r * q4) % SS
        m["rowidx"] = np.ascontiguousarray(vtok.reshape(nkb, 128).T.astype(np.int32))
        m["isr"] = ((np.arange(nkb) + r * (nkb // 4)) < nkb).astype(np.float32).reshape(1, nkb)
        m["flags"] = np.array([[1.0 if r > 0 else 0.0, 1.0 if r < 3 else 0.0,
                                0.0 if r > 0 else -BIG, 0.0 if r < 3 else -BIG]], np.float32)
        mv = np.zeros((128, 16, 128), np.float32)
        for g in range(4):
            mv[:, g * 4 + 0] = mm[:, g * 5 + 0] if r > 0 else 0.0
            mv[:, g * 4 + 1] = mm[:, g * 5 + 1] if r > 0 else mm[:, g * 5 + 3]
            mv[:, g * 4 + 2] = mm[:, g * 5 + 1] if r < 3 else mm[:, g * 5 + 4]
            mv[:, g * 4 + 3] = mm[:, g * 5 + 2] if r < 3 else 0.0
        m["mmatv"] = mv.astype(ml_dtypes.bfloat16)
        m["cosT2"] = np.ascontiguousarray(shared["cosT"][:, :SS][:, vtok])
        m["sinT2"] = np.ascontiguousarray(shared["sinT"][:, :SS][:, vtok])
        m["peT_s2"] = f(psm[DEPTH - 1, b, r * q4:(r + 1) * q4, :].T)
        maps.append(m)
    return maps


def kernel(**inp):
    SP, SS, DEPTH, ncores = CFG["SP"], CFG["SS"], CFG["DEPTH"], CFG["NCORES"]
    key = (SP, SS, DEPTH)
    maps = _prep_inputs(inp, SP, SS, DEPTH, ncores)
    nc = build_program(SP, SS, DEPTH)
    res = run_bass_kernel_spmd(nc, maps, core_ids=list(range(ncores)))
    nb_p = np.asarray(inp["x_prompt"]).shape[0]
    nb_s = np.asarray(inp["x_sample"]).shape[0]
    y_p = np.stack([np.asarray(res.results[i]["y_p"], dtype=np.float32) for i in range(nb_p)], 0)
    q4 = SS // 4
    y_s = np.zeros((nb_s, SS, D), np.float32)
    for i in range(ncores):
        b, r = i % nb_s, (i // nb_s) % 4
        if i // nb_s < 4:
            y_s[b, r * q4:(r + 1) * q4] = np.asarray(res.results[i]["y_s"], dtype=np.float32)
    return (y_p, y_s)
```
